# Optimizing a Trainium2 kernel written in Bass

```python
import jax, jax.numpy as jnp
from jax import lax
import numpy as np

D_MODEL = 2048
BATCH = 16
SEQ = 256
DEPTH = 2
DEC_BATCH = 8
DEC_SEQ = 4096
PAST_LEN = 512

GRID_W = 64
HEAD_DIM = 64
ATTN_DIM = D_MODEL // 2
N_HEADS = ATTN_DIM // HEAD_DIM
N_KV_HEADS = 4
Q_PER_KV = N_HEADS // N_KV_HEADS
KV_DIM = N_KV_HEADS * HEAD_DIM
WINDOW = 128
BLOCK = 128
CONV_DIM = D_MODEL // 4
CONV_K = 31
GMLP_DIM = D_MODEL // 4
GMLP_GROUP_DIM = 128
GMLP_GROUPS = GMLP_DIM // GMLP_GROUP_DIM
CHUNK = 128
MIX_DIM = ATTN_DIM + CONV_DIM + GMLP_DIM
IN_COLS = ATTN_DIM + 2 * KV_DIM + 2 * CONV_DIM + 2 * GMLP_DIM
D_FF = 5632
FFN_K = 3
ROPE_THETA = 10000.0
EPS = 1e-6
NEG = -1e30
SCALE = HEAD_DIM ** -0.5

kernel_name = "hybrid_prefix_diffusion_step"


def rmsnorm(x, g):
    xf = x.astype(jnp.float32)
    y = xf * lax.rsqrt(jnp.mean(xf * xf, axis=-1, keepdims=True) + EPS)
    return (y * g.astype(jnp.float32)).astype(x.dtype)


def layernorm(x, g, b):
    xf = x.astype(jnp.float32)
    mu = jnp.mean(xf, axis=-1, keepdims=True)
    var = jnp.mean(jnp.square(xf - mu), axis=-1, keepdims=True)
    y = (xf - mu) * lax.rsqrt(var + EPS) * g.astype(jnp.float32) + b.astype(jnp.float32)
    return y.astype(x.dtype)


def dwconv(x, w, b):
    pad = (w.shape[0] - 1) // 2
    y = lax.conv_general_dilated(x, w[:, None, :].astype(x.dtype), window_strides=(1,),
                                 padding=[(pad, pad)],
                                 dimension_numbers=('NWC', 'WIO', 'NWC'),
                                 feature_group_count=x.shape[-1])
    return y + b


def axial_rope(L):
    rows = L // GRID_W
    row = jnp.repeat(jnp.arange(rows, dtype=jnp.float32), GRID_W)
    col = jnp.tile(jnp.arange(GRID_W, dtype=jnp.float32), rows)
    n_freq = HEAD_DIM // 4
    inv = ROPE_THETA ** (-jnp.arange(n_freq, dtype=jnp.float32) / n_freq)
    ang = jnp.concatenate([row[:, None] * inv, col[:, None] * inv], axis=-1)
    return jnp.cos(ang), jnp.sin(ang)


def apply_rope(x, cos, sin):
    half = HEAD_DIM // 2
    c = cos[None, :, None, :].astype(x.dtype)
    s = sin[None, :, None, :].astype(x.dtype)
    x1, x2 = x[..., :half], x[..., half:]
    return jnp.concatenate([x1 * c - x2 * s, x1 * s + x2 * c], axis=-1)


def sink_softmax(s, sink):
    sk = sink.astype(jnp.float32).reshape(N_KV_HEADS, Q_PER_KV)[None, :, :, None, None]
    m = jnp.maximum(jnp.max(s, axis=-1, keepdims=True), sk)
    p = jnp.exp(s - m)
    return p / (jnp.sum(p, axis=-1, keepdims=True) + jnp.exp(sk - m))


def context_attention(q, k, v, sink):
    B, L = q.shape[0], q.shape[1]
    nb = L // BLOCK
    qb = q.reshape(B, nb, BLOCK, N_KV_HEADS, Q_PER_KV, HEAD_DIM).transpose(1, 0, 2, 3, 4, 5)

    def one(qi):
        s = jnp.einsum('bqkgd,bskd->bkgqs', qi, k).astype(jnp.float32) * SCALE
        p = sink_softmax(s, sink).astype(v.dtype)
        return jnp.einsum('bkgqs,bskd->bqkgd', p, v)

    o = lax.map(one, qb)
    return o.transpose(1, 0, 2, 3, 4, 5).reshape(B, L, ATTN_DIM)


def latent_attention(q, k, v, ck, cv, sink):
    B, L = q.shape[0], q.shape[1]
    nb = L // BLOCK
    qb = q.reshape(B, nb, BLOCK, N_KV_HEADS, Q_PER_KV, HEAD_DIM)
    pad = ((0, 0), (BLOCK, BLOCK), (0, 0), (0, 0))
    kp = jnp.pad(k, pad).reshape(B, nb + 2, BLOCK, N_KV_HEADS, HEAD_DIM)
    vp = jnp.pad(v, pad).reshape(B, nb + 2, BLOCK, N_KV_HEADS, HEAD_DIM)
    kw = jnp.concatenate([kp[:, :-2], kp[:, 1:-1], kp[:, 2:]], axis=2)
    vw = jnp.concatenate([vp[:, :-2], vp[:, 1:-1], vp[:, 2:]], axis=2)
    qpos = jnp.arange(BLOCK)
    kpos = jnp.arange(3 * BLOCK) - BLOCK
    band = jnp.abs(kpos[None, :] - qpos[:, None]) <= WINDOW

    def one(args):
        qi, ki, vi, n = args
        kabs = n * BLOCK + kpos
        valid = band & ((kabs >= 0) & (kabs < L))[None, :]
        s_loc = jnp.einsum('bqkgd,bskd->bkgqs', qi, ki).astype(jnp.float32) * SCALE
        s_loc = jnp.where(valid, s_loc, NEG)
        s_ctx = jnp.einsum('bqkgd,bskd->bkgqs', qi, ck).astype(jnp.float32) * SCALE
        p = sink_softmax(jnp.concatenate([s_loc, s_ctx], axis=-1), sink).astype(vi.dtype)
        return (jnp.einsum('bkgqs,bskd->bqkgd', p[..., :3 * BLOCK], vi)
                + jnp.einsum('bkgqs,bskd->bqkgd', p[..., 3 * BLOCK:], cv))

    xs = (qb.transpose(1, 0, 2, 3, 4, 5), kw.transpose(1, 0, 2, 3, 4),
          vw.transpose(1, 0, 2, 3, 4), jnp.arange(nb, dtype=jnp.int32))
    o = lax.map(one, xs)
    return o.transpose(1, 0, 2, 3, 4, 5).reshape(B, L, ATTN_DIM)


def conformer_conv(z, dw_w, dw_b, ln_g, ln_b):
    a, g = jnp.split(z, 2, axis=-1)
    h = dwconv(a * jax.nn.sigmoid(g), dw_w, dw_b)
    return jax.nn.silu(layernorm(h, ln_g, ln_b))


def chunk_gmlp(z, ln_g, ln_b, ws, bs):
    B, L = z.shape[0], z.shape[1]
    nc = L // CHUNK
    u, v = jnp.split(jax.nn.gelu(z), 2, axis=-1)
    v = layernorm(v.reshape(B, L, GMLP_GROUPS, GMLP_GROUP_DIM), ln_g, ln_b)
    v = v.reshape(B, nc, CHUNK, GMLP_GROUPS, GMLP_GROUP_DIM)
    sv = jnp.einsum('gpq,bnqgc->bnpgc', ws, v) + bs.T[:, :, None]
    return u * sv.reshape(B, L, GMLP_DIM)


def token_mixers(h, p, rope, ctx_k, ctx_v):
    B, L = h.shape[0], h.shape[1]
    z = h @ p['w_in']
    o1 = ATTN_DIM
    o2 = o1 + KV_DIM
    o3 = o2 + KV_DIM
    o4 = o3 + 2 * CONV_DIM
    q = z[..., :o1].reshape(B, L, N_HEADS, HEAD_DIM)
    k = z[..., o1:o2].reshape(B, L, N_KV_HEADS, HEAD_DIM)
    v = z[..., o2:o3].reshape(B, L, N_KV_HEADS, HEAD_DIM)
    if rope is None:
        attn = context_attention(q, k, v, p['attn_sink'])
    else:
        cos, sin = rope
        attn = latent_attention(apply_rope(q, cos, sin), apply_rope(k, cos, sin), v,
                                ctx_k, ctx_v, p['attn_sink'])
    conv = conformer_conv(z[..., o3:o4], p['conv_dw_w'], p['conv_dw_b'],
                          p['conv_ln_g'], p['conv_ln_b'])
    gm = chunk_gmlp(z[..., o4:], p['gmlp_ln_g'], p['gmlp_ln_b'], p['gmlp_ws'], p['gmlp_bs'])
    out = jnp.concatenate([attn, conv, gm], axis=-1) @ p['w_out']
    return out, k, v


def conv_ffn(h, w_up, dw_w, dw_b, w_down):
    a, b = jnp.split(dwconv(h @ w_up, dw_w, dw_b), 2, axis=-1)
    return (jax.nn.silu(a) * b) @ w_down


def trunk_layer(x, cond, p, rope, ctx_k, ctx_v):
    mod = (jax.nn.silu(cond) @ p['w_ada'] + p['b_ada'])[:, None, :]
    sh1, sc1, g1, sh2, sc2, g2 = jnp.split(mod, 6, axis=-1)
    h = rmsnorm(x, p['g_norm1']) * (1 + sc1) + sh1
    mix, k, v = token_mixers(h, p, rope, ctx_k, ctx_v)
    x = x + g1 * mix
    h = rmsnorm(x, p['g_norm2']) * (1 + sc2) + sh2
    x = x + g2 * conv_ffn(h, p['w_up'], p['ffn_dw_w'], p['ffn_dw_b'], p['w_down'])
    return x, k, v


def setup_inputs(seed: int = 0) -> dict:
    key = jax.random.key(seed)
    ks = jax.random.split(key, 32)
    nrm = lambda k, shape, s: jax.random.normal(k, shape, jnp.float32) * s
    return {
        'x_prompt': nrm(ks[0], (BATCH, SEQ, D_MODEL), 1.0),
        'x_sample': nrm(ks[1], (DEC_BATCH, DEC_SEQ, D_MODEL), 1.0),
        'cache_k': nrm(ks[2], (DEC_BATCH, DEPTH, PAST_LEN, N_KV_HEADS, HEAD_DIM), 1.0),
        'cache_v': nrm(ks[3], (DEC_BATCH, DEPTH, PAST_LEN, N_KV_HEADS, HEAD_DIM), 1.0),
        'c': nrm(ks[4], (DEC_BATCH, D_MODEL), 1.0),
        'c_ctx': nrm(ks[5], (D_MODEL,), 1.0),
        'w_ada': nrm(ks[6], (DEPTH, D_MODEL, 6 * D_MODEL), 0.5 * D_MODEL ** -0.5),
        'b_ada': nrm(ks[7], (DEPTH, 6 * D_MODEL), 0.02),
        'g_norm1': 1.0 + nrm(ks[8], (DEPTH, D_MODEL), 0.02),
        'w_in': nrm(ks[9], (DEPTH, D_MODEL, IN_COLS), D_MODEL ** -0.5),
        'attn_sink': nrm(ks[10], (DEPTH, N_HEADS), 1.0),
        'conv_dw_w': nrm(ks[11], (DEPTH, CONV_K, CONV_DIM), CONV_K ** -0.5),
        'conv_dw_b': nrm(ks[12], (DEPTH, CONV_DIM), 0.02),
        'conv_ln_g': 1.0 + nrm(ks[13], (DEPTH, CONV_DIM), 0.02),
        'conv_ln_b': nrm(ks[14], (DEPTH, CONV_DIM), 0.02),
        'gmlp_ln_g': 1.0 + nrm(ks[15], (DEPTH, GMLP_GROUPS, GMLP_GROUP_DIM), 0.02),
        'gmlp_ln_b': nrm(ks[16], (DEPTH, GMLP_GROUPS, GMLP_GROUP_DIM), 0.02),
        'gmlp_ws': nrm(ks[17], (DEPTH, GMLP_GROUPS, CHUNK, CHUNK), CHUNK ** -0.5),
        'gmlp_bs': 1.0 + nrm(ks[18], (DEPTH, GMLP_GROUPS, CHUNK), 0.02),
        'w_out': nrm(ks[19], (DEPTH, MIX_DIM, D_MODEL), MIX_DIM ** -0.5),
        'g_norm2': 1.0 + nrm(ks[20], (DEPTH, D_MODEL), 0.02),
        'w_up': nrm(ks[21], (DEPTH, D_MODEL, 2 * D_FF), D_MODEL ** -0.5),
        'ffn_dw_w': nrm(ks[22], (DEPTH, FFN_K, 2 * D_FF), FFN_K ** -0.5),
        'ffn_dw_b': nrm(ks[23], (DEPTH, 2 * D_FF), 0.02),
        'w_down': nrm(ks[24], (DEPTH, D_FF, D_MODEL), D_FF ** -0.5),
        'g_final': 1.0 + nrm(ks[25], (D_MODEL,), 0.02),
    }


def reference(x_prompt, x_sample, cache_k, cache_v, c, c_ctx, w_ada, b_ada, g_norm1, w_in,
              attn_sink, conv_dw_w, conv_dw_b, conv_ln_g, conv_ln_b, gmlp_ln_g, gmlp_ln_b,
              gmlp_ws, gmlp_bs, w_out, g_norm2, w_up, ffn_dw_w, ffn_dw_b, w_down, g_final):
    rope = axial_rope(x_sample.shape[1])
    cond_ctx = c_ctx[None, :]
    xp = x_prompt
    xs = x_sample
    new_ks = []
    new_vs = []
    for l in range(DEPTH):
        p = {
            'w_ada': w_ada[l], 'b_ada': b_ada[l], 'g_norm1': g_norm1[l], 'w_in': w_in[l],
            'attn_sink': attn_sink[l], 'conv_dw_w': conv_dw_w[l], 'conv_dw_b': conv_dw_b[l],
            'conv_ln_g': conv_ln_g[l], 'conv_ln_b': conv_ln_b[l], 'gmlp_ln_g': gmlp_ln_g[l],
            'gmlp_ln_b': gmlp_ln_b[l], 'gmlp_ws': gmlp_ws[l], 'gmlp_bs': gmlp_bs[l],
            'w_out': w_out[l], 'g_norm2': g_norm2[l], 'w_up': w_up[l],
            'ffn_dw_w': ffn_dw_w[l], 'ffn_dw_b': ffn_dw_b[l], 'w_down': w_down[l],
        }
        xp, kc, vc = trunk_layer(xp, cond_ctx, p, None, None, None)
        new_ks.append(kc)
        new_vs.append(vc)
        xs, _, _ = trunk_layer(xs, c, p, rope, cache_k[:, l], cache_v[:, l])
    y_prompt = rmsnorm(xp, g_final)
    y_sample = rmsnorm(xs, g_final)
    new_k = jnp.stack(new_ks, axis=1)
    new_v = jnp.stack(new_vs, axis=1)
    return (y_prompt, y_sample, new_k, new_v)
```

```python
import numpy as np
import concourse.bass as bass
import concourse.mybir as mybir
from concourse.bass_utils import run_bass_kernel_spmd

F32 = mybir.dt.float32
BF16 = mybir.dt.bfloat16
AF = mybir.ActivationFunctionType
ALU = mybir.AluOpType
AX = mybir.AxisListType

SAME_ENGINE_SYNC = True
NORM_ODD_ENG = "dve"
D = 2048
KC = 16
DFF = 5632
NJ = 44
LP = 256
EPS = 1e-6
SCALE = 0.125
TM = 512
TF = 512
NS = 130
WSLOT = 4096
NW = 6
PV_L = 624
NPV = 2 * PV_L + 16
NPR = 528


class Buf:
    __slots__ = ("name", "w", "r")

    def __init__(self, name=""):
        self.name = name
        self.w = None
        self.r = []


class FW:
    def __init__(self, nc, n_dma_sems=40):
        self.nc = nc
        self.engs = ["pe", "act", "dve", "pool", "sp"]
        self.sem = {}
        self.cnt = {}
        for e in self.engs:
            self.sem[e] = nc.alloc_semaphore(name="S_" + e)
            self.cnt[e] = 0
        self.dsem = [nc.alloc_semaphore(name="D_%d" % i) for i in range(n_dma_sems)]
        self.dcnt = [0] * n_dma_sems
        self.dnext = {"sp": 0, "pool": n_dma_sems // 2, "act": 0}
        self.drange = {"sp": (0, n_dma_sems // 2), "pool": (n_dma_sems // 2, n_dma_sems), "act": (0, n_dma_sems // 2)}
        for i, s in enumerate(self.dsem):
            self.sem[("d", i)] = s
        self.known = {e: {} for e in self.engs}
        self.n_instr = 0
        self.n_wait = 0
        self.prog = {e: [] for e in self.engs}

    def _need(self, e, deps):
        best = {}
        for d in deps:
            if d is None:
                continue
            k, v = d
            if k == e and (not SAME_ENGINE_SYNC or e == "pe" or v > self.cnt[e]):
                continue
            if best.get(k, 0) < v:
                best[k] = v
        kn = self.known[e]
        for k, v in best.items():
            if kn.get(k, 0) >= v:
                continue
            self.prog[e].append((0, self.sem[k], v))
            self.n_wait += 1
            kn[k] = v

    def op(self, e, name, reads=(), writes=(), inc=True, **kw):
        deps = []
        for b in reads:
            deps.append(b.w)
        for b in writes:
            deps.append(b.w)
            deps.extend(b.r)
        self._need(e, deps)
        self.n_instr += 1
        if inc:
            self.cnt[e] += 1
            self.prog[e].append((1, (name, kw), self.sem[e], 1))
            tok = (e, self.cnt[e])
        else:
            self.prog[e].append((1, (name, kw), None, 0))
            tok = (e, self.cnt[e] + 1)
        for b in reads:
            b.r.append(tok)
        for b in writes:
            b.w = tok
            b.r = []

    def dma(self, q, out, in_, reads=(), writes=(), **kw):
        deps = []
        for b in reads:
            deps.append(b.w)
        for b in writes:
            deps.append(b.w)
            deps.extend(b.r)
        self._need(q, deps)
        i = self.dnext[q]
        lo_, hi_ = self.drange[q]
        self.dnext[q] = lo_ + (i + 1 - lo_) % (hi_ - lo_)
        if self.dcnt[i] > 0:
            self._need(q, [(("d", i), self.dcnt[i])])
        kw = dict(kw)
        kw["out"] = out
        kw["in_"] = in_
        self.prog[q].append((1, ("dma_start", kw), self.dsem[i], 16))
        self.dcnt[i] += 16
        self.n_instr += 1
        tok = (("d", i), self.dcnt[i])
        for b in reads:
            b.r.append(tok)
        for b in writes:
            b.w = tok
            b.r = []

    def barrier(self):
        deps = [(("d", i), self.dcnt[i]) for i in range(len(self.dsem)) if self.dcnt[i]]
        for e in self.engs:
            if self.cnt[e]:
                deps.append((e, self.cnt[e]))
        for e in self.engs:
            self._need(e, deps)

    def finish(self):
        deps = [(("d", i), self.dcnt[i]) for i in range(len(self.dsem)) if self.dcnt[i]]
        for e in self.engs:
            if e != "sp" and self.cnt[e]:
                deps.append((e, self.cnt[e]))
        self._need("sp", deps)

    def emit(self):
        nc = self.nc
        prog = self.prog

        def run(eng, lst):
            for it in lst:
                if it[0] == 0:
                    eng.wait_ge(it[1], it[2])
                else:
                    ins = getattr(eng, it[1][0])(**it[1][1])
                    if it[2] is not None:
                        ins.then_inc(it[2], it[3])

        with nc.Block() as block:
            @block.sync
            def _(eng):
                run(eng, prog["sp"])

            @block.tensor
            def _(eng):
                run(eng, prog["pe"])

            @block.scalar
            def _(eng):
                run(eng, prog["act"])

            @block.vector
            def _(eng):
                run(eng, prog["dve"])

            @block.gpsimd
            def _(eng):
                run(eng, prog["pool"])


def v3(ap2, c):
    return ap2.rearrange("p (c t) -> p c t", c=c)


class Rot:
    def __init__(self, items):
        self.items = items
        self.i = 0

    def get(self):
        it = self.items[self.i]
        self.i = (self.i + 1) % len(self.items)
        return it


def build(LS):
    NT = LS + 2 * LP
    nc = bass.Bass("TRN2", target_bir_lowering=False)
    fw = FW(nc)

    def din(name, shape):
        return nc.dram_tensor(name, list(shape), F32, kind="ExternalInput").ap()

    def dout(name, shape):
        return nc.dram_tensor(name, list(shape), F32, kind="ExternalOutput").ap()

    xs_d = din("xs", [LS, D])
    xp_d = din("xp", [2 * LP, D])
    ck_d = din("ck", [2, 512, 256])
    cv_d = din("cv", [2, 512, 256])
    ccT_d = din("ccT", [128, 32])
    pvec_d = din("pvec", [128, NPV])
    prow_d = din("prow", [2, NPR])
    w_ada_d = din("w_ada", [2, D, 6 * D])
    w_in_d = din("w_in", [2, D, 3584])
    w_out_d = din("w_out", [2, D, D])
    w_up_d = din("w_up", [2, D, 2 * DFF])
    w_down_d = din("w_down", [2, DFF, D])
    ws_d = din("gmlp_ws", [2, 4, 128, 128])
    ident_d = din("ident", [128, 128])
    perm_d = din("perm", [128, 128])
    ropeC_d = din("ropeC", [128, LS])
    ropeS_d = din("ropeS", [128, LS])
    maskP_d = din("maskP", [128, 512])
    maskN_d = din("maskN", [128, 512])
    ys_d = dout("ys", [LS, D])
    yp_d = dout("yp", [2 * LP, D])
    nk_d = dout("nk", [2, 2, LP, 256])
    nv_d = dout("nv", [2, 2, LP, 256])
    XA = nc.dram_tensor("XA", [KC, 128, NT], F32).ap().rearrange("c p t -> p c t")
    XB = nc.dram_tensor("XB", [KC, 128, NT], F32).ap().rearrange("c p t -> p c t")
    BXA = [[Buf() for _ in range(KC)] for _ in range(NT // 128 + 1)]
    BXB = [[Buf() for _ in range(KC)] for _ in range(NT // 128 + 1)]

    def blkbufs(B, t0, n, oc=None):
        out = []
        for bl in B[t0 // 128:(t0 + n - 1) // 128 + 1]:
            out += (bl if oc is None else [bl[oc]])
        return out

    def sb(name, shape, dt):
        return nc.alloc_sbuf_tensor("s_" + name, list(shape), dt)

    pv = sb("pv", [128, NPV], F32); Bpv = Buf()
    ident = sb("ident", [128, 128], F32); Bident = Buf()
    perm = sb("perm", [128, 128], F32); Bperm = Buf()
    ones_bf = sb("ones_bf", [128, 128], BF16); Bones = Buf()
    ones_f = sb("ones_f", [128, 128], F32)
    zo = sb("zo", [1, 128], BF16)
    epsT = sb("epsT", [128, 1], F32)
    maskP = sb("maskP", [128, 128], BF16); BmaskP = Buf()
    maskN = sb("maskN", [128, 128], BF16); BmaskN = Buf()
    ident_bf = sb("ident_bf", [128, 128], BF16)
    dgb = Rot([(sb("dg%d" % i, [128, 8 * 128], BF16), Buf()) for i in range(2)])
    scT = sb("scT", [128, 32], BF16); BscT = Buf()
    ccs = sb("ccs", [128, 32], F32); Bccs = Buf()
    mod = [sb("mod%d" % l, [128, 192], F32) for l in range(2)]; Bmod = [Buf(), Buf()]
    der = [sb("der%d" % l, [128, 2 * 32], F32) for l in range(2)]; Bder = [Buf(), Buf()]
    wsT = sb("wsT", [128, 4 * 128], BF16); BwsT = Buf()
    Bmat = sb("Bmat", [128, 4 * 128], F32); BBmat = Buf()
    sinkhl = sb("sinkhl", [1, 64], BF16); Bsink = Buf()
    sinkf = sb("sinkf", [1, 64], F32)
    KcT = sb("KcT", [128, 2 * 512], BF16); BKcT = Buf()
    Vc = sb("Vc", [128, 4 * 4 * 128], BF16); BVc = Buf()
    wring = Rot([(sb("wr%d" % i, [128, WSLOT], BF16), Buf("wr%d" % i)) for i in range(NW)])
    xst = Rot([(sb("xst%d" % i, [128, KC * NS], F32), Buf()) for i in range(2)])
    sqb = sb("sqb", [128, KC * NS], BF16); Bsqb = Buf()
    rsb = Rot([(sb("rsb%d" % i, [128, NS], F32), Buf()) for i in range(2)])
    BIG = 82 * 1024
    big = sb("big", [128, BIG // 2], BF16)
    tmpf_items = [(sb("tmpf%d" % i, [128, 520], F32), Buf()) for i in range(7)]
    tmpf = Rot(tmpf_items)
    tmpC = Rot(tmpf_items[0:2])
    tmpG = Rot(tmpf_items[4:7])
    xres = Rot([(sb("xres%d" % i, [128, 512], F32), Buf()) for i in range(2)])
    xoutb = Rot([(sb("xout%d" % i, [128, 512], F32), Buf()) for i in range(2)])
    pT = Rot([(sb("pT%d" % i, [128, 512], BF16), Buf()) for i in range(7)])
    smallf = sb("smallf", [128, 64], F32); Bsmall = Buf()
    ps = [nc.alloc_psum_tensor("ps%d" % i, [128, 512], F32) for i in range(8)]
    Bps = [Buf("ps%d" % i) for i in range(8)]
    psAll = Rot([(ps[i], Bps[i]) for i in range(8)])
    psS = Rot([(ps[i], Bps[i]) for i in range(3)])
    psC = Rot([(ps[i], Bps[i]) for i in range(4)])
    psO = Rot([(ps[i], Bps[i]) for i in (4, 5)])
    rcb = Rot([(sb("rcb%d" % i, [64, 512], F32), Buf()) for i in range(1)])

    def bigv(off_bytes, nbytes, dt):
        if dt == BF16:
            return big[:, off_bytes // 2:(off_bytes + nbytes) // 2]
        return big[:, off_bytes // 2:(off_bytes + nbytes) // 2].bitcast(F32)

    trst = Rot([(bigv(i * 8192, 8192, F32), Buf()) for i in range(2)])
    ckst = bigv(0, 4096, F32); Bckst = Buf()
    wsTf = bigv(4096, 2048, F32)
    o = 0
    h1 = bigv(o, KC * 768 * 2, BF16); o += KC * 768 * 2
    kT = bigv(o, 2 * 768 * 2, BF16); o += 2 * 768 * 2
    Vx = bigv(o, 6 * 4 * 128 * 2, BF16); o += 6 * 4 * 128 * 2
    qT = bigv(o, 8 * 512 * 2, BF16); o += 8 * 512 * 2
    cacc = bigv(o, 4 * 512 * 4, F32); o += 4 * 512 * 4
    mixT = bigv(o, KC * 512 * 2, BF16); o += KC * 512 * 2
    gu = bigv(o, 4 * 512 * 2, BF16); o += 4 * 512 * 2
    nTok = bigv(o, 4 * 512 * 2, BF16); o += 4 * 512 * 2
    glub = Rot([(bigv(o + i * 544 * 2, 544 * 2, BF16), Buf()) for i in range(2)]); o += 2 * 544 * 2
    ropC = bigv(o, 768 * 4, F32); o += 768 * 4
    ropS = bigv(o, 768 * 4, F32); o += 768 * 4
    assert o <= BIG, o
    BkT, BVx, BqT, Bgu, Brop = [Buf() for _ in range(5)]
    BnTok = [Buf() for _ in range(4)]
    Bh1 = [Buf() for _ in range(KC)]
    BmixT = [Buf() for _ in range(KC)]
    Bcacc = [Buf() for _ in range(4)]
    o = 0
    h2 = bigv(o, KC * (TF + 4) * 2, BF16); o += KC * (TF + 4) * 2
    actT = bigv(o, NJ * TF * 2, BF16); o += NJ * TF * 2
    assert o <= BIG, o
    Bh2 = [Buf() for _ in range(KC)]
    Bact = [Buf() for _ in range(NJ)]

    h1v = v3(h1, KC)
    kTv = v3(kT, 2)
    Vxv = Vx.rearrange("p (b k d) -> p b k d", b=6, k=4)
    qTv = v3(qT, 8)
    mixTv = v3(mixT, KC)
    guv = v3(gu, 4)
    nTokv = nTok.rearrange("p (b g c) -> p b g c", b=4, g=4)
    caccv = v3(cacc, 4)
    h2v = v3(h2, KC)
    actv = v3(actT, NJ)
    Vcv = Vc[:, :].rearrange("p (b k d) -> p b k d", b=4, k=4)
    KcTv = v3(KcT[:, :], 2)

    def pvc(l, off, n=1):
        b = l * PV_L + off
        return pv[:, b:b + n]

    def wload(src, ncols, kc=KC):
        t, B = wring.get()
        fw.dma("pool", v3(t[:, 0:kc * ncols], kc), src, writes=[B])
        return v3(t[:, 0:kc * ncols], kc), B

    def w_cols(wd, l, c0, w):
        return wd[l].rearrange("(kc p) n -> p kc n", p=128)[:, :, c0:c0 + w]

    def norm_L(X, BX, tok0, n):
        st, Bst = xst.get()
        st3 = v3(st[:, :], KC)
        fw.dma("sp", st3[:, :, 0:n], X[:, :, tok0:tok0 + n], reads=blkbufs(BX, tok0, n), writes=[Bst])
        return (st3, Bst, n)

    def norm_A(X, BX, tok0, n, pre=None):
        st3, Bst, n = pre if pre is not None else norm_L(X, BX, tok0, n)
        sq3 = v3(sqb[:, :], KC)
        fw.op("act", "activation", out=sq3[:, :, 0:n], in_=st3[:, :, 0:n], func=AF.Square, reads=[Bst], writes=[Bsqb])
        p, Bp = psAll.get()
        for c in range(KC):
            fw.op("pe", "matmul", out=p[:, 0:n], lhsT=ones_bf[:, :], rhs=sq3[:, c, 0:n], start=(c == 0), stop=(c == KC - 1),
                  reads=[Bsqb, Bones], writes=[Bp], inc=(c == KC - 1))
        return (st3, Bst, p, Bp, n)

    def norm_B(ctx, l, r, which, outfn, Bout):
        st3, Bst, p, Bp, n = ctx
        rs, Brs = rsb.get()
        fw.op("act", "activation", out=rs[:, 0:n], in_=p[:, 0:n], func=AF.Sqrt, scale=1.0 / D, bias=epsT[:, 0:1], reads=[Bp], writes=[Brs])
        fw.op("dve", "reciprocal", out=rs[:, 0:n], in_=rs[:, 0:n], reads=[Brs], writes=[Brs])
        fw.op("dve", "tensor_tensor", out=st3[:, :, 0:n], in0=st3[:, :, 0:n], in1=rs[:, 0:n].unsqueeze(1).broadcast_to([128, KC, n]),
              op=ALU.mult, reads=[Bst, Brs], writes=[Bst])
        for c in range(KC):
            if which == 2:
                fw.op("dve", "tensor_scalar", out=outfn(c), in0=st3[:, c, 0:n], scalar1=pv[:, 2 * PV_L + c:2 * PV_L + c + 1], scalar2=None,
                      op0=ALU.mult, reads=[Bst, Bpv], writes=[Bout[c] if isinstance(Bout, list) else Bout])
                continue
            A = der[l][:, r * 32 + which * 16 + c:r * 32 + which * 16 + c + 1]
            Bsh = mod[l][:, ((0 if which == 0 else 48) + c) * 2 + r:((0 if which == 0 else 48) + c) * 2 + r + 1]
            if c % 2 == 0:
                fw.op("act", "activation", out=outfn(c), in_=st3[:, c, 0:n], func=AF.Identity, scale=A, bias=Bsh,
                      reads=[Bst, Bder[l], Bmod[l]], writes=[Bout[c] if isinstance(Bout, list) else Bout])
            else:
                fw.op(NORM_ODD_ENG, "tensor_scalar", out=outfn(c), in0=st3[:, c, 0:n], scalar1=A, scalar2=Bsh, op0=ALU.mult, op1=ALU.add,
                      reads=[Bst, Bder[l], Bmod[l]], writes=[Bout[c] if isinstance(Bout, list) else Bout])

    def rmsnorm(X, BX, tok0, n, l, r, which, outfn, Bout):
        norm_B(norm_A(X, BX, tok0, n), l, r, which, outfn, Bout)

    def norm_steps(X, BX, subs, l, r, which, Bout):
        pre = [norm_L(X, BX, subs[j][0], subs[j][1]) for j in range(min(2, len(subs)))]
        yield "P"
        ctx = norm_A(X, BX, subs[0][0], subs[0][1], pre[0])
        yield
        for j in range(len(subs)):
            nxt = None
            if j + 1 < len(subs):
                nxt = norm_A(X, BX, subs[j + 1][0], subs[j + 1][1], pre[1] if j == 0 else None)
                yield
            norm_B(ctx, l, r, which, subs[j][2], Bout)
            yield
            ctx = nxt

    def proj(wt, Bw, wcol0, act3, Bact_, slices, kc=KC, rot=None, extra_reads=()):
        outs = []
        for (c0, n) in slices:
            p, Bp = (rot or psAll).get()
            outs.append((p, Bp, n))
        for k in range(kc):
            for si, (c0, n) in enumerate(slices):
                p, Bp, _ = outs[si]
                last = (k == kc - 1)
                fw.op("pe", "matmul", out=p[:, 0:n], lhsT=wt[:, k, wcol0:wcol0 + 128], rhs=act3[:, k, c0:c0 + n],
                      start=(k == 0), stop=last, reads=[Bw, (Bact_[k] if isinstance(Bact_, list) else Bact_)] + list(extra_reads), writes=[Bp], inc=last)
        return outs

    fw.dma("sp", pv[:, :], pvec_d, writes=[Bpv])
    fw.dma("sp", ident[:, :], ident_d, writes=[Bident])
    fw.dma("sp", perm[:, :], perm_d, writes=[Bperm])
    fw.dma("sp", ccs[:, :], ccT_d, writes=[Bccs])
    fw.dma("pool", maskP[:, :], maskP_d[:, 0:128], writes=[BmaskP])
    fw.dma("pool", maskN[:, :], maskN_d[:, 0:128], writes=[BmaskN])
    fw.op("dve", "tensor_copy", out=ident_bf[:, :], in_=ident[:, :], reads=[Bident], writes=[Bident])
    fw.op("dve", "memset", ap=ones_bf[:, :], constant=1.0, writes=[Bones])
    fw.op("dve", "memset", ap=ones_f[:, :], constant=1.0, writes=[Bones])
    fw.op("dve", "memset", ap=zo[:, 0:64], constant=0.0, writes=[Bones])
    fw.op("dve", "memset", ap=zo[:, 64:128], constant=1.0, writes=[Bones])
    fw.op("dve", "memset", ap=epsT[:, :], constant=EPS, writes=[Bones])
    fw.op("dve", "memset", ap=Vc[:, :], constant=1.0, writes=[BVc])
    fw.op("act", "activation", out=scT[:, :], in_=ccs[:, :], func=AF.Silu, reads=[Bccs], writes=[BscT])

    def ada_gen():
        for l in range(2):
            pm, Bpm = ps[7 - l], Bps[7 - l]
            for cg in range(48):
                wt, Bw = wload(w_cols(w_ada_d, l, cg * 256, 256), 256)
                for oc in range(2):
                    j = cg * 2 + oc
                    for kc in range(KC):
                        fw.op("pe", "matmul", out=pm[:, 2 * j:2 * j + 2], lhsT=wt[:, kc, oc * 128:(oc + 1) * 128], rhs=scT[:, 2 * kc:2 * kc + 2],
                              start=(kc == 0), stop=(kc == KC - 1), reads=[Bw, BscT], writes=[Bpm], inc=(kc == KC - 1))
                yield
            fw.op("dve", "tensor_tensor", out=mod[l][:, :].rearrange("p (j r) -> p j r", r=2), in0=pm[:, 0:192].rearrange("p (j r) -> p j r", r=2),
                  in1=pvc(l, 32, 96).unsqueeze(2).broadcast_to([128, 96, 2]), op=ALU.add, reads=[Bpm, Bpv], writes=[Bmod[l]])
            m3 = mod[l][:, :].rearrange("p (j r) -> p j r", r=2)
            for r in range(2):
                for which in range(2):
                    sc = m3[:, (16 if which == 0 else 64):(32 if which == 0 else 80), r]
                    dst = der[l][:, r * 32 + which * 16:r * 32 + which * 16 + 16]
                    fw.op("dve", "tensor_scalar", out=dst, in0=sc, scalar1=1.0, scalar2=None, op0=ALU.add, reads=[Bmod[l]], writes=[Bder[l]])
                    fw.op("dve", "tensor_tensor", out=dst, in0=dst, in1=pvc(l, 0 if which == 0 else 16, 16), op=ALU.mult,
                          reads=[Bder[l], Bpv], writes=[Bder[l]])
            yield

    psI = Rot([(ps[i], Bps[i]) for i in range(6)])

    def init_gen():
        for blk in range(NT // 128):
            t0 = blk * 128
            src = xs_d[t0:t0 + 128, :] if t0 < LS else xp_d[t0 - LS:t0 - LS + 128, :]
            st, Bst = trst.get()
            fw.dma("sp", st[:, :], src, writes=[Bst])
            so, Bso = xst.get()
            so3 = v3(so[:, 0:KC * 128], KC)
            for b4 in range(4):
                p, Bp = psI.get()
                for q in range(4):
                    c = b4 * 4 + q
                    fw.op("pe", "transpose", out=p[:, q * 128:(q + 1) * 128], in_=st[:, c * 128:(c + 1) * 128], identity=ident[:, :],
                          reads=[Bst, Bident], writes=[Bp], inc=(q == 3))
                if b4 % 2 == 0:
                    fw.op("act", "copy", out=so[:, b4 * 512:(b4 + 1) * 512], in_=p[:, :], reads=[Bp], writes=[Bso])
                else:
                    fw.op("dve", "tensor_copy", out=so[:, b4 * 512:(b4 + 1) * 512], in_=p[:, :], reads=[Bp], writes=[Bso])
            fw.dma("sp", XA[:, :, t0:t0 + 128], so3, reads=[Bso], writes=blkbufs(BXA, t0, 128))
            yield

    ga_, gi_ = ada_gen(), init_gen()
    a_alive = i_alive = True
    while a_alive or i_alive:
        for _ in range(3):
            if a_alive:
                try:
                    next(ga_)
                except StopIteration:
                    a_alive = False
        if i_alive:
            try:
                next(gi_)
            except StopIteration:
                i_alive = False

    def layer_setup(l):
        prow, Bprow = bigv(8192, 4096, F32), Buf()
        fw.dma("sp", prow[0:1, 0:NPR], prow_d[l:l + 1, :], writes=[Bprow])
        for g in range(4):
            st, Bst = tmpf.get()
            fw.dma("sp", st[:, 0:128], ws_d[l, g], writes=[Bst])
            p, Bp = psAll.get()
            fw.op("pe", "transpose", out=p[:, 0:128], in_=st[:, 0:128], identity=ident[:, :], reads=[Bst, Bident], writes=[Bp])
            fw.op("act", "copy", out=wsT[:, g * 128:(g + 1) * 128], in_=p[:, 0:128], reads=[Bp], writes=[BwsT])
            fw.op("dve", "tensor_copy", out=wsTf[:, g * 128:(g + 1) * 128], in_=p[:, 0:128], reads=[Bp], writes=[BwsT])
            p2, Bp2 = psAll.get()
            fw.op("pe", "matmul", out=p2[:, 0:128], lhsT=ones_f[:, :], rhs=wsTf[:, g * 128:(g + 1) * 128], start=True, stop=True,
                  reads=[BwsT, Bones], writes=[Bp2])
            p3, Bp3 = psAll.get()
            fw.op("pe", "matmul", out=p3[:, 0:128], lhsT=ones_f[0:1, :], rhs=prow[0:1, g * 128:(g + 1) * 128],
                  start=True, stop=True, reads=[Bprow, Bones], writes=[Bp3])
            t2, Bt2 = tmpf.get()
            fw.op("act", "copy", out=t2[:, 0:128], in_=p3[:, 0:128], reads=[Bp3], writes=[Bt2])
            fw.op("dve", "scalar_tensor_tensor", out=Bmat[:, g * 128:(g + 1) * 128], in0=p2[:, 0:128], scalar=pvc(l, 620 + g), in1=t2[:, 0:128],
                  op0=ALU.mult, op1=ALU.add, reads=[Bp2, Bt2, Bpv], writes=[BBmat])
        fw.op("act", "activation", out=sinkf[:, 0:16], in_=prow[0:1, 512:528], func=AF.Exp, reads=[Bprow], writes=[Bsink])
        fw.op("dve", "tensor_copy", out=sinkhl[:, 0:16], in_=sinkf[:, 0:16], reads=[Bsink], writes=[Bsink])
        fw.op("dve", "tensor_copy", out=sinkf[:, 16:32], in_=sinkhl[:, 0:16], reads=[Bsink], writes=[Bsink])
        fw.op("dve", "tensor_tensor", out=sinkf[:, 32:48], in0=sinkf[:, 0:16], in1=sinkf[:, 16:32], op=ALU.subtract, reads=[Bsink], writes=[Bsink])
        fw.op("dve", "tensor_copy", out=sinkhl[:, 16:32], in_=sinkf[:, 32:48], reads=[Bsink], writes=[Bsink])
        fw.dma("sp", ckst[:, :].rearrange("p (b f) -> p b f", b=4), ck_d[l].rearrange("(b p) f -> p b f", p=128), writes=[Bckst])
        for b in range(4):
            for ch in range(2):
                p, Bp = psAll.get()
                fw.op("pe", "transpose", out=p[:, 0:128], in_=ckst[:, b * 256 + ch * 128:b * 256 + (ch + 1) * 128], identity=ident[:, :],
                      reads=[Bckst, Bident], writes=[Bp])
                fw.op("act", "copy", out=KcTv[:, ch, b * 128:(b + 1) * 128], in_=p[:, 0:128], reads=[Bp], writes=[BKcT])
        for b in range(4):
            fw.dma("pool", Vcv[:, b, :, 0:64], cv_d[l, b * 128:(b + 1) * 128, :].rearrange("p (k d) -> p k d", k=4), writes=[BVc])

    def mixer_tile(l, kind, s, T, seq0, L, pr):
        r = kind
        sample = (kind == 0)
        e = s + T
        has_l = s > 0
        has_r = e < L
        lo = s - 128 if has_l else s
        hi = e + 128 if has_r else e
        col = lambda t: t - (s - 128)
        nb = T // 128
        subs = []
        t = lo
        while t < hi:
            n = min(128, hi - t)
            subs.append((seq0 + t, n, (lambda c, t=t, n=n: h1v[:, c, col(t):col(t) + n])))
            t += n
        for tk in norm_steps(XA, BXA, subs, l, r, 0, Bh1):
            yield tk or "N"
        yield "ENDNORM"
        if sample:
            fw.dma("sp", ropC[:, col(lo):col(hi)], ropeC_d[:, lo:hi], writes=[Brop])
            fw.dma("sp", ropS[:, col(lo):col(hi)], ropeS_d[:, lo:hi], writes=[Brop])

        def rope_evac(p, Bp, n, c0, dst, Bdst):
            kf, Bkf = tmpf.get()
            fw.op("act", "copy", out=kf[:, 0:n], in_=p[:, 0:n], reads=[Bp], writes=[Bkf])
            p2, Bp2 = psAll.get()
            fw.op("pe", "matmul", out=p2[:, 0:n], lhsT=perm[:, :], rhs=kf[:, 0:n], start=True, stop=True, reads=[Bkf, Bperm], writes=[Bp2])
            t1, Bt1 = tmpf.get()
            fw.op("dve", "tensor_tensor", out=t1[:, 0:n], in0=kf[:, 0:n], in1=ropC[:, c0:c0 + n], op=ALU.mult, reads=[Bkf, Brop], writes=[Bt1])
            t2, Bt2 = tmpf.get()
            fw.op("dve", "tensor_tensor", out=t2[:, 0:n], in0=p2[:, 0:n], in1=ropS[:, c0:c0 + n], op=ALU.mult, reads=[Bp2, Brop], writes=[Bt2])
            fw.op("dve", "tensor_tensor", out=dst, in0=t1[:, 0:n], in1=t2[:, 0:n], op=ALU.add, reads=[Bt1, Bt2], writes=[Bdst])

        wk, Bwk = wload(w_cols(w_in_d, l, 1024, 256), 256)
        wv, Bwv = wload(w_cols(w_in_d, l, 1280, 256), 256)
        nall = hi - lo
        kslices = []
        t = lo
        while t < hi:
            n = min(384, hi - t)
            kslices.append((col(t), n))
            t += n
        for ch in range(2):
            outs = proj(wk, Bwk, ch * 128, h1v, Bh1, kslices)
            for (p, Bp, n), (c0, _) in zip(outs, kslices):
                if sample:
                    rope_evac(p, Bp, n, c0, kTv[:, ch, c0:c0 + n], BkT)
                else:
                    fw.op("act", "copy", out=kTv[:, ch, c0:c0 + n], in_=p[:, 0:n], reads=[Bp], writes=[BkT])
        for b in range((hi - lo) // 128):
            c0 = col(lo) + b * 128
            bi = c0 // 128
            p, Bp = psAll.get()
            for k in range(KC):
                fw.op("pe", "matmul", out=p[:, 0:256], lhsT=h1v[:, k, c0:c0 + 128], rhs=wv[:, k, 0:256], start=(k == 0), stop=(k == KC - 1),
                      reads=[Bh1[k], Bwv], writes=[Bp], inc=(k == KC - 1))
            fw.op("act", "copy", out=Vxv[:, bi, :, 0:64], in_=p[:, 0:256].rearrange("p (k d) -> p k d", k=4), reads=[Bp], writes=[BVx])
            fw.op("dve", "memset", ap=Vxv[:, bi, :, 64:128], constant=1.0, writes=[BVx])
            if not sample:
                to, Bto = xoutb.get()
                fw.op("dve", "tensor_copy", out=to[:, 0:256], in_=p[:, 0:256], reads=[Bp], writes=[Bto])
                fw.dma("sp", nv_d[pr, l, s + b * 128:s + (b + 1) * 128, :], to[:, 0:256], reads=[Bto], writes=[Buf()])
                p2, Bp2 = psAll.get()
                for k in range(KC):
                    fw.op("pe", "matmul", out=p2[:, 0:256], lhsT=h1v[:, k, c0:c0 + 128], rhs=wk[:, k, 0:256], start=(k == 0), stop=(k == KC - 1),
                          reads=[Bh1[k], Bwk], writes=[Bp2], inc=(k == KC - 1))
                to2, Bto2 = xoutb.get()
                fw.op("act", "copy", out=to2[:, 0:256], in_=p2[:, 0:256], reads=[Bp2], writes=[Bto2])
                fw.dma("sp", nk_d[pr, l, s + b * 128:s + (b + 1) * 128, :], to2[:, 0:256], reads=[Bto2], writes=[Buf()])
        cs = col(s)
        for half in range(4):
            wq, Bwq = wload(w_cols(w_in_d, l, half * 256, 256), 256)
            for cc in range(2):
                ch = half * 2 + cc
                (p, Bp, n), = proj(wq, Bwq, cc * 128, h1v, Bh1, [(cs, T)])
                if sample:
                    rope_evac(p, Bp, T, cs, qTv[:, ch, 0:T], BqT)
                else:
                    fw.op("act", "copy", out=qTv[:, ch, 0:T], in_=p[:, 0:T], reads=[Bp], writes=[BqT])
        def attention_gen():
            for qb in range(nb):
                for kv in range(4):
                    plo = (kv % 2) * 64
                    kch = kv // 2
                    qc0 = (kv // 2) * 4
                    keys = []
                    if sample:
                        gb = (s // 128) + qb
                        for dlt in (-1, 0, 1):
                            g2 = gb + dlt
                            if g2 < 0 or g2 >= L // 128:
                                continue
                            c0 = col(g2 * 128)
                            keys.append((kTv[plo:plo + 64, kch, c0:c0 + 128], Vxv[:, c0 // 128, kv, :], dlt, [BkT], [BVx]))
                        for b in range(4):
                            keys.append((KcTv[plo:plo + 64, kch, b * 128:(b + 1) * 128], Vcv[:, b, kv, :], 0, [BKcT], [BVc]))
                    else:
                        for b in range(L // 128):
                            c0 = col(b * 128)
                            keys.append((kTv[plo:plo + 64, kch, c0:c0 + 128], Vxv[:, c0 // 128, kv, :], 0, [BkT], [BVx]))
                    rhs_q = qTv[plo:plo + 64, qc0:qc0 + 4, qb * 128:(qb + 1) * 128]
                    pts = []
                    for (kap, vap, dlt, kr, vr) in keys:
                        p, Bp = psS.get()
                        fw.op("pe", "matmul", out=p[:, :], lhsT=kap, rhs=rhs_q, start=True, stop=True, reads=kr + [BqT], writes=[Bp])
                        pt, Bpt = pT.get()
                        fw.op("act", "activation", out=pt[:, :], in_=p[:, :], func=AF.Exp, scale=SCALE, reads=[Bp], writes=[Bpt])
                        if dlt == -1:
                            fw.op("dve", "tensor_tensor", out=pt[:, :].rearrange("p (g t) -> p g t", g=4), in0=pt[:, :].rearrange("p (g t) -> p g t", g=4), in1=maskP[:, :].unsqueeze(1).broadcast_to([128, 4, 128]), op=ALU.mult, reads=[Bpt, BmaskP], writes=[Bpt])
                        elif dlt == 1:
                            fw.op("dve", "tensor_tensor", out=pt[:, :].rearrange("p (g t) -> p g t", g=4), in0=pt[:, :].rearrange("p (g t) -> p g t", g=4), in1=maskN[:, :].unsqueeze(1).broadcast_to([128, 4, 128]), op=ALU.mult, reads=[Bpt, BmaskN], writes=[Bpt])
                        pts.append((pt, Bpt, vap, vr))
                    yield
                    po, Bpo = psO.get()
                    for i, (pt, Bpt, vap, vr) in enumerate(pts):
                        fw.op("pe", "matmul", out=po[:, :], lhsT=vap, rhs=pt[:, :], start=(i == 0), stop=False, reads=vr + [Bpt], writes=[Bpo], inc=False)
                    for hl in range(2):
                        fw.op("pe", "matmul", out=po[:, :], lhsT=zo[0:1, :],
                              rhs=sinkhl[0:1, hl * 16 + kv * 4:hl * 16 + kv * 4 + 4].unsqueeze(2).broadcast_to([1, 4, 128]),
                              start=False, stop=(hl == 1), reads=[Bsink, Bones], writes=[Bpo], inc=(hl == 1))
                    rc, Brc = rcb.get()
                    fw.op("dve", "reciprocal", out=rc[0:64, 0:512], in_=po[64:128, :], reads=[Bpo], writes=[Brc])
                    po4 = po[0:64, :].rearrange("p (a b t) -> p a b t", a=2, b=2)
                    rc4 = rc[0:64, 0:512].rearrange("p (a b t) -> p a b t", a=2, b=2)
                    for par in range(2):
                        fw.op("dve", "tensor_tensor", out=mixTv[par * 64:par * 64 + 64, kv * 2:kv * 2 + 2, qb * 128:(qb + 1) * 128],
                              in0=po4[:, :, par, :], in1=rc4[:, :, par, :], op=ALU.mult, reads=[Bpo, Brc], writes=BmixT[kv * 2:kv * 2 + 2])
                    yield

        def conv_gen():
            clo = max(s - 15, 0) if has_l else s
            chi = min(e + 15, L) if has_r else e
            gcol = lambda t: t - (s - 15)
            cslices = []
            t = clo
            ncs = (chi - clo + 511) // 512
            for i in range(ncs):
                n = (chi - t + (ncs - i) - 1) // (ncs - i)
                cslices.append((col(t), n, gcol(t)))
                t += n
            wa2 = [wload(w_cols(w_in_d, l, 1536 + i * 256, 256), 256) for i in range(2)]
            wg2 = [wload(w_cols(w_in_d, l, 2048 + i * 256, 256), 256) for i in range(2)]
            p1, Bp1 = ps[6], Bps[6]
            p2, Bp2 = ps[7], Bps[7]
            for c in range(4):
                gl, Bglu = glub.get()
                if not has_l:
                    fw.op("dve", "memset", ap=gl[:, 0:15], constant=0.0, writes=[Bglu])
                if not has_r:
                    fw.op("dve", "memset", ap=gl[:, 15 + T:30 + T], constant=0.0, writes=[Bglu])
                (wa, Bwa), (wg, Bwg) = wa2[c // 2], wg2[c // 2]
                oa = proj(wa, Bwa, (c % 2) * 128, h1v, Bh1, [(c0, n) for (c0, n, _) in cslices], rot=psC)
                og = proj(wg, Bwg, (c % 2) * 128, h1v, Bh1, [(c0, n) for (c0, n, _) in cslices], rot=psC)
                for (pa, Bpa, n), (pg, Bpg, _), (_, _, g0) in zip(oa, og, cslices):
                    sg, Bsg = tmpC.get()
                    fw.op("act", "activation", out=sg[:, 0:n], in_=pg[:, 0:n], func=AF.Sigmoid, reads=[Bpg], writes=[Bsg])
                    fw.op("dve", "tensor_tensor", out=gl[:, g0:g0 + n], in0=pa[:, 0:n], in1=sg[:, 0:n], op=ALU.mult, reads=[Bpa, Bsg], writes=[Bglu])
                yield
                pc, Bpc = ps[3], Bps[3]
                for k0 in range(0, 31, 8):
                    dg, Bdg = dgb.get()
                    kn = min(8, 31 - k0)
                    for j in range(kn):
                        if j % 2 == 0:
                            fw.op("act", "activation", out=dg[:, j * 128:(j + 1) * 128], in_=ident_bf[:, :], func=AF.Copy, scale=pvc(l, 480 + (k0 + j) * 4 + c),
                                  reads=[Bident, Bpv], writes=[Bdg])
                        else:
                            fw.op("dve", "tensor_scalar", out=dg[:, j * 128:(j + 1) * 128], in0=ident_bf[:, :], scalar1=pvc(l, 480 + (k0 + j) * 4 + c), scalar2=None,
                                  op0=ALU.mult, reads=[Bident, Bpv], writes=[Bdg])
                    for j in range(kn):
                        k = k0 + j
                        fw.op("pe", "matmul", out=pc[:, 0:T], lhsT=dg[:, j * 128:(j + 1) * 128], rhs=gl[:, k:k + T], start=(k == 0), stop=(k == 30),
                              reads=[Bdg, Bglu], writes=[Bpc], inc=(j == kn - 1))
                    yield
                fw.op("act", "activation", out=caccv[:, c, 0:T], in_=pc[:, 0:T], func=AF.Identity, scale=1.0, bias=pvc(l, 604 + c),
                      reads=[Bpc, Bpv], writes=[Bcacc[c]])
                sq, Bsq = tmpC.get()
                fw.op("act", "activation", out=sq[:, 0:T], in_=caccv[:, c, 0:T], func=AF.Square, reads=[Bcacc[c]], writes=[Bsq])
                fw.op("pe", "matmul", out=p1[:, 0:T], lhsT=ones_f[:, :], rhs=caccv[:, c, 0:T], start=(c == 0), stop=(c == 3), reads=[Bcacc[c], Bones],
                      writes=[Bp1])
                fw.op("pe", "matmul", out=p2[:, 0:T], lhsT=ones_f[:, :], rhs=sq[:, 0:T], start=(c == 0), stop=(c == 3), reads=[Bsq, Bones],
                      writes=[Bp2])
                yield
            mu, Bmu = tmpf_items[2]
            fw.op("act", "activation", out=mu[:, 0:T], in_=p1[:, 0:T], func=AF.Identity, scale=1.0 / 512, reads=[Bp1], writes=[Bmu])
            msq, Bmsq = tmpf_items[3]
            fw.op("dve", "tensor_tensor", out=msq[:, 0:T], in0=mu[:, 0:T], in1=mu[:, 0:T], op=ALU.mult, reads=[Bmu], writes=[Bmsq])
            fw.op("dve", "scalar_tensor_tensor", out=msq[:, 0:T], in0=p2[:, 0:T], scalar=1.0 / 512, in1=msq[:, 0:T], op0=ALU.mult, op1=ALU.subtract,
                  reads=[Bp2, Bmsq], writes=[Bmsq])
            fw.op("act", "activation", out=msq[:, 0:T], in_=msq[:, 0:T], func=AF.Sqrt, scale=1.0, bias=epsT[:, 0:1], reads=[Bmsq], writes=[Bmsq])
            fw.op("dve", "reciprocal", out=msq[:, 0:T], in_=msq[:, 0:T], reads=[Bmsq], writes=[Bmsq])
            yield
            for c in range(4):
                fw.op("dve", "tensor_tensor", out=caccv[:, c, 0:T], in0=caccv[:, c, 0:T], in1=mu[:, 0:T], op=ALU.subtract, reads=[Bcacc[c], Bmu], writes=[Bcacc[c]])
                fw.op("dve", "tensor_tensor", out=caccv[:, c, 0:T], in0=caccv[:, c, 0:T], in1=msq[:, 0:T], op=ALU.mult, reads=[Bcacc[c], Bmsq], writes=[Bcacc[c]])
                fw.op("act", "activation", out=mixTv[:, 8 + c, 0:T], in_=caccv[:, c, 0:T], func=AF.Silu, scale=pvc(l, 608 + c), bias=pvc(l, 612 + c),
                      reads=[Bcacc[c], Bpv], writes=[BmixT[8 + c]])
                yield

        def gmlp_gen():
            wgu2 = [wload(w_cols(w_in_d, l, 2560 + i * 256, 256), 256) for i in range(2)]
            wgv2 = [wload(w_cols(w_in_d, l, 3072 + i * 256, 256), 256) for i in range(2)]
            for c in range(4):
                wgu, Bwgu = wgu2[c // 2]
                (p, Bp, n), = proj(wgu, Bwgu, (c % 2) * 128, h1v, Bh1, [(cs, T)], rot=psS)
                fw.op("act", "activation", out=guv[:, c, 0:T], in_=p[:, 0:T], func=AF.Gelu_apprx_tanh, reads=[Bp], writes=[Bgu])
                yield
            for b in range(nb):
                c0 = cs + b * 128
                p, Bp = psS.get()
                for hv in range(2):
                    wgv, Bwgv = wgv2[hv]
                    for k in range(KC):
                        fw.op("pe", "matmul", out=p[:, hv * 256:(hv + 1) * 256], lhsT=h1v[:, k, c0:c0 + 128], rhs=wgv[:, k, :], start=(k == 0),
                              stop=(k == KC - 1), reads=[Bh1[k], Bwgv], writes=[Bp], inc=(k == KC - 1))
                gvt, Bgvt = tmpG.get()
                fw.op("act", "activation", out=gvt[:, 0:512], in_=p[:, :], func=AF.Gelu_apprx_tanh, reads=[Bp], writes=[Bgvt])
                g2, Bg2 = tmpG.get()
                fw.op("act", "activation", out=g2[:, 0:512], in_=gvt[:, 0:512], func=AF.Square, reads=[Bgvt], writes=[Bg2])
                yield
                sm = smallf
                fw.op("dve", "reduce_sum", out=sm[:, 0:4], in_=gvt[:, 0:512].rearrange("p (g c) -> p g c", g=4), axis=AX.X, reads=[Bgvt], writes=[Bsmall])
                fw.op("dve", "reduce_sum", out=sm[:, 4:8], in_=g2[:, 0:512].rearrange("p (g c) -> p g c", g=4), axis=AX.X, reads=[Bg2], writes=[Bsmall])
                fw.op("dve", "tensor_scalar", out=sm[:, 0:8], in0=sm[:, 0:8], scalar1=1.0 / 128, scalar2=None, op0=ALU.mult, reads=[Bsmall], writes=[Bsmall])
                fw.op("dve", "tensor_tensor", out=sm[:, 8:12], in0=sm[:, 0:4], in1=sm[:, 0:4], op=ALU.mult, reads=[Bsmall], writes=[Bsmall])
                fw.op("dve", "tensor_tensor", out=sm[:, 8:12], in0=sm[:, 4:8], in1=sm[:, 8:12], op=ALU.subtract, reads=[Bsmall], writes=[Bsmall])
                fw.op("act", "activation", out=sm[:, 8:12], in_=sm[:, 8:12], func=AF.Sqrt, scale=1.0, bias=epsT[:, 0:1], reads=[Bsmall], writes=[Bsmall])
                fw.op("dve", "reciprocal", out=sm[:, 8:12], in_=sm[:, 8:12], reads=[Bsmall], writes=[Bsmall])
                sp_, Bsp = psS.get()
                for g in range(4):
                    fw.op("dve", "tensor_scalar", out=nTokv[:, b, g, :], in0=gvt[:, g * 128:(g + 1) * 128], scalar1=sm[:, g:g + 1], scalar2=sm[:, 8 + g:9 + g],
                          op0=ALU.subtract, op1=ALU.mult, reads=[Bgvt, Bsmall], writes=[BnTok[g]])
                for g in range(4):
                    fw.op("pe", "matmul", out=sp_[:, g * 128:(g + 1) * 128], lhsT=nTokv[:, b, g, :], rhs=wsT[:, g * 128:(g + 1) * 128], start=True, stop=True,
                          reads=[BnTok[g], BwsT], writes=[Bsp])
                sv, Bsv = tmpG.get()
                for g in range(4):
                    fw.op("dve", "scalar_tensor_tensor", out=sv[:, g * 128:(g + 1) * 128], in0=sp_[:, g * 128:(g + 1) * 128], scalar=pvc(l, 616 + g),
                          in1=Bmat[:, g * 128:(g + 1) * 128], op0=ALU.mult, op1=ALU.add, reads=[Bsp, BBmat, Bpv], writes=[Bsv])
                fw.op("dve", "tensor_tensor", out=mixTv[:, 12:16, b * 128:(b + 1) * 128], in0=guv[:, :, b * 128:(b + 1) * 128],
                      in1=sv[:, 0:512].rearrange("p (g t) -> p g t", g=4), op=ALU.mult, reads=[Bgu, Bsv], writes=BmixT[12:16])
                yield

        def step(g_):
            try:
                next(g_)
                return True
            except StopIteration:
                return False

        ga, gg, gc = attention_gen(), gmlp_gen(), conv_gen()
        a_alive = True
        while step(gg):
            if a_alive:
                a_alive = step(ga)
        c_alive = True
        while a_alive or c_alive:
            if a_alive:
                a_alive = step(ga)
            if c_alive:
                c_alive = step(gc)
        yield "PREOUT"
        for og in range(8):
            wo, Bwo = wload(w_cols(w_out_d, l, og * 256, 256), 256)
            for cc in range(2):
                oc = og * 2 + cc
                xr, Bxr = xres.get()
                fw.dma("sp", xr[:, 0:T], XA[:, oc, seq0 + s:seq0 + e], reads=blkbufs(BXA, seq0 + s, T, oc), writes=[Bxr])
                (p, Bp, n), = proj(wo, Bwo, cc * 128, mixTv, BmixT, [(0, T)])
                xo, Bxo = xoutb.get()
                fw.op("dve", "scalar_tensor_tensor", out=xo[:, 0:T], in0=p[:, 0:T], scalar=mod[l][:, (32 + oc) * 2 + r:(32 + oc) * 2 + r + 1],
                      in1=xr[:, 0:T], op0=ALU.mult, op1=ALU.add, reads=[Bp, Bxr, Bmod[l]], writes=[Bxo])
                fw.dma("sp", XB[:, oc, seq0 + s:seq0 + e], xo[:, 0:T], reads=[Bxo], writes=blkbufs(BXB, seq0 + s, T, oc))
                yield "O"

    def ffn_tile(l, kind, segs):
        r = kind
        T = sum(sg[1] for sg in segs)
        nseg = len(segs)
        offs = []
        o = 0
        for (x0, n, hl, hr) in segs:
            offs.append(o)
            o += n + 2
        W = o
        first = True
        subs = []
        for (x0, n, hl, hr), uo in zip(segs, offs):
            lo = x0 - 1 if hl else x0
            hi = x0 + n + 1 if hr else x0 + n
            if not hl:
                fw.op("dve", "memset", ap=h2v[:, :, uo:uo + 1], constant=0.0, writes=Bh2)
            if not hr:
                fw.op("dve", "memset", ap=h2v[:, :, uo + n + 1:uo + n + 2], constant=0.0, writes=Bh2)
            first = False
            t = lo
            while t < hi:
                m = min(NS, hi - t)
                cbase = uo + 1 + (t - x0)
                subs.append((t, m, (lambda c, cbase=cbase, m=m: h2v[:, c, cbase:cbase + m])))
                t += m
        for tk in norm_steps(XB, BXB, subs, l, r, 1, Bh2):
            yield tk or "N"
        yield "ENDNORM"
        nsl = (W + 511) // 512
        slices = []
        t = 0
        for i in range(nsl):
            m = (W - t + (nsl - i) - 1) // (nsl - i)
            slices.append((t, m))
            t += m
        for jg in range(NJ // 2):
            wa, Bwa = wload(w_cols(w_up_d, l, jg * 256, 256), 256)
            wb, Bwb = wload(w_cols(w_up_d, l, DFF + jg * 256, 256), 256)
            for jj in range(2):
                j = jg * 2 + jj
                res = []
                for half, (wt, Bw) in enumerate(((wa, Bwa), (wb, Bwb))):
                    outs = proj(wt, Bw, jj * 128, h2v, Bh2, slices)
                    U, BU = tmpf.get()
                    for (p, Bp, m), (c0, _) in zip(outs, slices):
                        fw.op("act", "copy", out=U[:, c0:c0 + m], in_=p[:, 0:m], reads=[Bp], writes=[BU])
                    acc, Bacc = tmpf.get()
                    ch = half * NJ + j
                    for (x0, n, hl, hr), uo in zip(segs, offs):
                        ao = uo - 2 * segs.index((x0, n, hl, hr))
                        fw.op("dve", "tensor_scalar", out=acc[:, ao:ao + n], in0=U[:, uo + 1:uo + 1 + n], scalar1=pvc(l, 128 + 88 + ch),
                              scalar2=pvc(l, 392 + ch), op0=ALU.mult, op1=ALU.add, reads=[BU, Bpv], writes=[Bacc])
                        fw.op("dve", "scalar_tensor_tensor", out=acc[:, ao:ao + n], in0=U[:, uo:uo + n], scalar=pvc(l, 128 + ch), in1=acc[:, ao:ao + n],
                              op0=ALU.mult, op1=ALU.add, reads=[BU, Bpv, Bacc], writes=[Bacc])
                        fw.op("dve", "scalar_tensor_tensor", out=acc[:, ao:ao + n], in0=U[:, uo + 2:uo + 2 + n], scalar=pvc(l, 128 + 176 + ch),
                              in1=acc[:, ao:ao + n], op0=ALU.mult, op1=ALU.add, reads=[BU, Bpv, Bacc], writes=[Bacc])
                    res.append((acc, Bacc))
                (aa, Baa), (ab, Bab) = res
                fw.op("act", "activation", out=aa[:, 0:T], in_=aa[:, 0:T], func=AF.Silu, reads=[Baa], writes=[Baa])
                fw.op("dve", "tensor_tensor", out=actv[:, j, 0:T], in0=aa[:, 0:T], in1=ab[:, 0:T], op=ALU.mult, reads=[Baa, Bab], writes=[Bact[j]])
        yield "PREOUT"
        for oc in range(KC):
            wdh = [wload(w_down_d[l].rearrange("(kc p) n -> p kc n", p=128)[:, hk * 22:(hk + 1) * 22, oc * 128:(oc + 1) * 128], 128, kc=22)
                   for hk in range(2)]
            xr, Bxr = xres.get()
            a0 = 0
            for (x0, n, hl, hr) in segs:
                fw.dma("sp", xr[:, a0:a0 + n], XB[:, oc, x0:x0 + n], reads=blkbufs(BXB, x0, n, oc), writes=[Bxr])
                a0 += n
            p, Bp = psAll.get()
            for k in range(NJ):
                wd, Bwd = wdh[k // 22]
                fw.op("pe", "matmul", out=p[:, 0:T], lhsT=wd[:, k % 22, 0:128], rhs=actv[:, k, 0:T], start=(k == 0), stop=(k == NJ - 1),
                      reads=[Bwd, Bact[k]], writes=[Bp], inc=(k == NJ - 1))
            xo, Bxo = xoutb.get()
            fw.op("dve", "scalar_tensor_tensor", out=xo[:, 0:T], in0=p[:, 0:T], scalar=mod[l][:, (80 + oc) * 2 + r:(80 + oc) * 2 + r + 1],
                  in1=xr[:, 0:T], op0=ALU.mult, op1=ALU.add, reads=[Bp, Bxr, Bmod[l]], writes=[Bxo])
            a0 = 0
            for (x0, n, hl, hr) in segs:
                fw.dma("sp", XA[:, oc, x0:x0 + n], xo[:, a0:a0 + n], reads=[Bxo], writes=blkbufs(BXA, x0, n, oc))
                a0 += n
            yield "O"

    def drive(tile_gens):
        prev = None
        for g in tile_gens:
            tk = next(g)
            assert tk == "P", tk
            while True:
                if prev is not None:
                    try:
                        next(prev)
                    except StopIteration:
                        prev = None
                if next(g) == "ENDNORM":
                    break
            if prev is not None:
                for _ in prev:
                    pass
            while next(g) != "PREOUT":
                pass
            prev = g
        if prev is not None:
            for _ in prev:
                pass

    for l in range(2):
        fw.barrier()
        layer_setup(l)
        fw.barrier()
        tiles = [mixer_tile(l, 0, s, TM, 0, LS, 0) for s in range(0, LS, TM)]
        tiles += [mixer_tile(l, 1, 0, LP, LS + pr * LP, LP, pr) for pr in range(2)]
        drive(tiles)
        fw.barrier()
        tiles = [ffn_tile(l, 0, [(s, TF, s > 0, s + TF < LS)]) for s in range(0, LS, TF)]
        tiles.append(ffn_tile(l, 1, [(LS, LP, False, False), (LS + LP, LP, False, False)]))
        drive(tiles)
    fw.barrier()

    nblk = NT // 128

    def fin_stage1(blk):
        return norm_A(XA, BXA, blk * 128, 128)

    def fin_stage2(blk, ctx):
        t0 = blk * 128
        st3, Bst, p_, Bp_, n_ = ctx
        norm_B(ctx, 0, 0, 2, (lambda c: st3[:, c, 0:128]), Bst)
        so, Bso = trst.get()
        for b4 in range(4):
            p, Bp = psAll.get()
            for q in range(4):
                c = b4 * 4 + q
                fw.op("pe", "transpose", out=p[:, q * 128:(q + 1) * 128], in_=st3[:, c, 0:128], identity=ident[:, :], reads=[Bst, Bident], writes=[Bp],
                      inc=(q == 3))
            if b4 % 2 == 0:
                fw.op("act", "copy", out=so[:, b4 * 512:(b4 + 1) * 512], in_=p[:, :], reads=[Bp], writes=[Bso])
            else:
                fw.op("dve", "tensor_copy", out=so[:, b4 * 512:(b4 + 1) * 512], in_=p[:, :], reads=[Bp], writes=[Bso])
        dst = ys_d[t0:t0 + 128, :] if t0 < LS else yp_d[t0 - LS:t0 - LS + 128, :]
        fw.dma("sp", dst, so[:, :], reads=[Bso], writes=[Buf()])

    ctx = fin_stage1(0)
    for blk in range(nblk):
        nxt = fin_stage1(blk + 1) if blk + 1 < nblk else None
        fin_stage2(blk, ctx)
        ctx = nxt

    fw.finish()
    fw.emit()
    return nc, fw


def _qperm():
    order = []
    for grp in range(2):
        for m in range(4):
            order += [grp * 8 + m, grp * 8 + 4 + m]
    cols = []
    for h in order:
        cols += list(range(h * 64, (h + 1) * 64))
    return np.array(cols + list(range(1024, 3584)), dtype=np.int64)


def _fm(v):
    v = np.asarray(v, dtype=np.float32)
    n = v.shape[-1] // 128
    v = v.reshape(v.shape[:-1] + (n, 128))
    return np.moveaxis(v, -1, 0)


def _consts(LS):
    ident = np.eye(128, dtype=np.float32)
    perm = np.zeros((128, 128), np.float32)
    for m in range(128):
        partner = m + 32 if (m % 64) < 32 else m - 32
        perm[partner, m] = 1.0
    rows = LS // 64
    row = np.repeat(np.arange(rows, dtype=np.float32), 64)
    colv = np.tile(np.arange(64, dtype=np.float32), rows)
    inv = (np.float32(10000.0) ** (-np.arange(16, dtype=np.float32) / np.float32(16))).astype(np.float32)
    ang = np.concatenate([row[:, None] * inv, colv[:, None] * inv], axis=-1).astype(np.float32)
    cos = np.cos(ang).astype(np.float32)
    sin = np.sin(ang).astype(np.float32)
    C = np.zeros((128, LS), np.float32)
    S = np.zeros((128, LS), np.float32)
    for p in range(128):
        d = p % 64
        C[p] = cos[:, d % 32]
        S[p] = -sin[:, d] if d < 32 else sin[:, d - 32]
    j = np.arange(128)[:, None]
    i = np.arange(128)[None, :]
    mP = np.tile((j >= i).astype(np.float32), (1, 4))
    mN = np.tile((j <= i).astype(np.float32), (1, 4))
    return dict(ident=ident, perm=perm, ropeC=C, ropeS=S, maskP=mP, maskN=mN)


def _pack(inp):
    pvec = np.zeros((128, NPV), np.float32)
    for l in range(2):
        b = l * PV_L
        pvec[:, b + 0:b + 16] = _fm(inp["g_norm1"][l])
        pvec[:, b + 16:b + 32] = _fm(inp["g_norm2"][l])
        pvec[:, b + 32:b + 128] = _fm(inp["b_ada"][l])
        pvec[:, b + 128:b + 392] = _fm(inp["ffn_dw_w"][l]).reshape(128, 3 * 88)
        pvec[:, b + 392:b + 480] = _fm(inp["ffn_dw_b"][l])
        pvec[:, b + 480:b + 604] = _fm(inp["conv_dw_w"][l]).reshape(128, 31 * 4)
        pvec[:, b + 604:b + 608] = _fm(inp["conv_dw_b"][l])
        pvec[:, b + 608:b + 612] = _fm(inp["conv_ln_g"][l])
        pvec[:, b + 612:b + 616] = _fm(inp["conv_ln_b"][l])
        pvec[:, b + 616:b + 620] = np.asarray(inp["gmlp_ln_g"][l], np.float32).T
        pvec[:, b + 620:b + 624] = np.asarray(inp["gmlp_ln_b"][l], np.float32).T
    pvec[:, 2 * PV_L:2 * PV_L + 16] = _fm(inp["g_final"])
    prow = np.zeros((2, NPR), np.float32)
    for l in range(2):
        prow[l, 0:512] = np.asarray(inp["gmlp_bs"][l], np.float32).reshape(-1)
        prow[l, 512:528] = np.asarray(inp["attn_sink"][l], np.float32).reshape(-1)
    return pvec, prow


_CACHE = {}


def kernel(**inp):
    inp = {k: np.asarray(v) for k, v in inp.items()}
    LS = inp["x_sample"].shape[1]
    n = 8
    if LS not in _CACHE:
        _CACHE[LS] = build(LS)[0]
    nc = _CACHE[LS]
    consts = _consts(LS)
    pvec, prow = _pack(inp)
    w_in = np.ascontiguousarray(inp["w_in"][:, :, _qperm()])
    shared = dict(pvec=pvec, prow=prow, w_ada=inp["w_ada"], w_in=w_in, w_out=inp["w_out"], w_up=inp["w_up"], w_down=inp["w_down"],
                  gmlp_ws=inp["gmlp_ws"], **consts)
    in_maps = []
    for c in range(n):
        cc = np.stack([inp["c"][c], inp["c_ctx"]], 0).astype(np.float32)
        ccT = np.ascontiguousarray(cc.reshape(2, 16, 128).transpose(2, 1, 0).reshape(128, 32))
        m = dict(shared)
        m.update(xs=np.ascontiguousarray(inp["x_sample"][c]),
                 xp=np.ascontiguousarray(inp["x_prompt"][2 * c:2 * c + 2].reshape(2 * LP, D)),
                 ck=np.ascontiguousarray(inp["cache_k"][c].reshape(2, 512, 256)),
                 cv=np.ascontiguousarray(inp["cache_v"][c].reshape(2, 512, 256)),
                 ccT=ccT)
        in_maps.append(m)
    res = run_bass_kernel_spmd(nc, in_maps, core_ids=list(range(n))).results
    y_sample = np.stack([r["ys"] for r in res], 0)
    y_prompt = np.concatenate([r["yp"].reshape(2, LP, D) for r in res], 0)
    nk = np.concatenate([r["nk"] for r in res], 0).reshape(16, 2, LP, 4, 64)
    nv = np.concatenate([r["nv"] for r in res], 0).reshape(16, 2, LP, 4, 64)
    return (y_prompt.astype(np.float32), y_sample.astype(np.float32), nk.astype(np.float32), nv.astype(np.float32))
```

```python
import numpy as np
import concourse.bass as bass
import concourse.mybir as mybir
from concourse.bass_utils import run_bass_kernel_spmd

F32 = mybir.dt.float32
BF16 = mybir.dt.bfloat16
AF = mybir.ActivationFunctionType
ALU = mybir.AluOpType
AX = mybir.AxisListType

SAME_ENGINE_SYNC = True
NORM_ODD_ENG = "dve"
D = 2048
KC = 16
DFF = 5632
NJ = 44
LP = 256
EPS = 1e-6
SCALE = 0.125
TM = 512
TF = 512
NS = 130
WSLOT = 4096
NW = 6
PV_L = 624
NPV = 2 * PV_L + 16
NPR = 528


class Buf:
    __slots__ = ("name", "w", "r")

    def __init__(self, name=""):
        self.name = name
        self.w = None
        self.r = []


class FW:
    def __init__(self, nc, n_dma_sems=40):
        self.nc = nc
        self.engs = ["pe", "act", "dve", "pool", "sp"]
        self.sem = {}
        self.cnt = {}
        for e in self.engs:
            self.sem[e] = nc.alloc_semaphore(name="S_" + e)
            self.cnt[e] = 0
        self.dsem = [nc.alloc_semaphore(name="D_%d" % i) for i in range(n_dma_sems)]
        self.dcnt = [0] * n_dma_sems
        self.dnext = {"sp": 0, "pool": n_dma_sems // 2, "act": 0}
        self.drange = {"sp": (0, n_dma_sems // 2), "pool": (n_dma_sems // 2, n_dma_sems), "act": (0, n_dma_sems // 2)}
        for i, s in enumerate(self.dsem):
            self.sem[("d", i)] = s
        self.known = {e: {} for e in self.engs}
        self.n_instr = 0
        self.n_wait = 0
        self.prog = {e: [] for e in self.engs}

    def _need(self, e, deps):
        best = {}
        for d in deps:
            if d is None:
                continue
            k, v = d
            if k == e and (not SAME_ENGINE_SYNC or e == "pe" or v > self.cnt[e]):
                continue
            if best.get(k, 0) < v:
                best[k] = v
        kn = self.known[e]
        for k, v in best.items():
            if kn.get(k, 0) >= v:
                continue
            self.prog[e].append((0, self.sem[k], v))
            self.n_wait += 1
            kn[k] = v

    def op(self, e, name, reads=(), writes=(), inc=True, **kw):
        deps = []
        for b in reads:
            deps.append(b.w)
        for b in writes:
            deps.append(b.w)
            deps.extend(b.r)
        self._need(e, deps)
        self.n_instr += 1
        if inc:
            self.cnt[e] += 1
            self.prog[e].append((1, (name, kw), self.sem[e], 1))
            tok = (e, self.cnt[e])
        else:
            self.prog[e].append((1, (name, kw), None, 0))
            tok = (e, self.cnt[e] + 1)
        for b in reads:
            b.r.append(tok)
        for b in writes:
            b.w = tok
            b.r = []

    def dma(self, q, out, in_, reads=(), writes=(), **kw):
        deps = []
        for b in reads:
            deps.append(b.w)
        for b in writes:
            deps.append(b.w)
            deps.extend(b.r)
        self._need(q, deps)
        i = self.dnext[q]
        lo_, hi_ = self.drange[q]
        self.dnext[q] = lo_ + (i + 1 - lo_) % (hi_ - lo_)
        if self.dcnt[i] > 0:
            self._need(q, [(("d", i), self.dcnt[i])])
        kw = dict(kw)
        kw["out"] = out
        kw["in_"] = in_
        self.prog[q].append((1, ("dma_start", kw), self.dsem[i], 16))
        self.dcnt[i] += 16
        self.n_instr += 1
        tok = (("d", i), self.dcnt[i])
        for b in reads:
            b.r.append(tok)
        for b in writes:
            b.w = tok
            b.r = []

    def barrier(self):
        deps = [(("d", i), self.dcnt[i]) for i in range(len(self.dsem)) if self.dcnt[i]]
        for e in self.engs:
            if self.cnt[e]:
                deps.append((e, self.cnt[e]))
        for e in self.engs:
            self._need(e, deps)

    def finish(self):
        deps = [(("d", i), self.dcnt[i]) for i in range(len(self.dsem)) if self.dcnt[i]]
        for e in self.engs:
            if e != "sp" and self.cnt[e]:
                deps.append((e, self.cnt[e]))
        self._need("sp", deps)

    def emit(self):
        nc = self.nc
        prog = self.prog

        def run(eng, lst):
            for it in lst:
                if it[0] == 0:
                    eng.wait_ge(it[1], it[2])
                else:
                    ins = getattr(eng, it[1][0])(**it[1][1])
                    if it[2] is not None:
                        ins.then_inc(it[2], it[3])

        with nc.Block() as block:
            @block.sync
            def _(eng):
                run(eng, prog["sp"])

            @block.tensor
            def _(eng):
                run(eng, prog["pe"])

            @block.scalar
            def _(eng):
                run(eng, prog["act"])

            @block.vector
            def _(eng):
                run(eng, prog["dve"])

            @block.gpsimd
            def _(eng):
                run(eng, prog["pool"])


def v3(ap2, c):
    return ap2.rearrange("p (c t) -> p c t", c=c)


class Rot:
    def __init__(self, items):
        self.items = items
        self.i = 0

    def get(self):
        it = self.items[self.i]
        self.i = (self.i + 1) % len(self.items)
        return it


def build(LS):
    NT = LS + 2 * LP
    nc = bass.Bass("TRN2", target_bir_lowering=False)
    fw = FW(nc)

    def din(name, shape):
        return nc.dram_tensor(name, list(shape), F32, kind="ExternalInput").ap()

    def dout(name, shape):
        return nc.dram_tensor(name, list(shape), F32, kind="ExternalOutput").ap()

    xs_d = din("xs", [LS, D])
    xp_d = din("xp", [2 * LP, D])
    ck_d = din("ck", [2, 512, 256])
    cv_d = din("cv", [2, 512, 256])
    ccT_d = din("ccT", [128, 32])
    pvec_d = din("pvec", [128, NPV])
    prow_d = din("prow", [2, NPR])
    w_ada_d = din("w_ada", [2, D, 6 * D])
    w_in_d = din("w_in", [2, D, 3584])
    w_out_d = din("w_out", [2, D, D])
    w_up_d = din("w_up", [2, D, 2 * DFF])
    w_down_d = din("w_down", [2, DFF, D])
    ws_d = din("gmlp_ws", [2, 4, 128, 128])
    ident_d = din("ident", [128, 128])
    perm_d = din("perm", [128, 128])
    ropeC_d = din("ropeC", [128, LS])
    ropeS_d = din("ropeS", [128, LS])
    maskP_d = din("maskP", [128, 512])
    maskN_d = din("maskN", [128, 512])
    ys_d = dout("ys", [LS, D])
    yp_d = dout("yp", [2 * LP, D])
    nk_d = dout("nk", [2, 2, LP, 256])
    nv_d = dout("nv", [2, 2, LP, 256])
    XA = nc.dram_tensor("XA", [KC, 128, NT], F32).ap().rearrange("c p t -> p c t")
    XB = nc.dram_tensor("XB", [KC, 128, NT], F32).ap().rearrange("c p t -> p c t")
    BXA = [[Buf() for _ in range(KC)] for _ in range(NT // 128 + 1)]
    BXB = [[Buf() for _ in range(KC)] for _ in range(NT // 128 + 1)]

    def blkbufs(B, t0, n, oc=None):
        out = []
        for bl in B[t0 // 128:(t0 + n - 1) // 128 + 1]:
            out += (bl if oc is None else [bl[oc]])
        return out

    def sb(name, shape, dt):
        return nc.alloc_sbuf_tensor("s_" + name, list(shape), dt)

    pv = sb("pv", [128, NPV], F32); Bpv = Buf()
    ident = sb("ident", [128, 128], F32); Bident = Buf()
    perm = sb("perm", [128, 128], F32); Bperm = Buf()
    ones_bf = sb("ones_bf", [128, 128], BF16); Bones = Buf()
    ones_f = sb("ones_f", [128, 128], F32)
    zo = sb("zo", [1, 128], BF16)
    epsT = sb("epsT", [128, 1], F32)
    maskP = sb("maskP", [128, 128], BF16); BmaskP = Buf()
    maskN = sb("maskN", [128, 128], BF16); BmaskN = Buf()
    ident_bf = sb("ident_bf", [128, 128], BF16)
    dgb = Rot([(sb("dg%d" % i, [128, 8 * 128], BF16), [Buf() for _ in range(8)]) for i in range(2)])
    scT = sb("scT", [128, 32], BF16); BscT = Buf()
    ccs = sb("ccs", [128, 32], F32); Bccs = Buf()
    mod = [sb("mod%d" % l, [128, 192], F32) for l in range(2)]; Bmod = [Buf(), Buf()]
    der = [sb("der%d" % l, [128, 2 * 32], F32) for l in range(2)]; Bder = [Buf(), Buf()]
    wsT = sb("wsT", [128, 4 * 128], BF16); BwsT = Buf()
    Bmat = sb("Bmat", [128, 4 * 128], F32); BBmat = Buf()
    sinkhl = sb("sinkhl", [1, 64], BF16); Bsink = Buf()
    sinkf = sb("sinkf", [1, 64], F32)
    KcT = sb("KcT", [128, 2 * 512], BF16); BKcT = Buf()
    Vc = sb("Vc", [128, 4 * 4 * 128], BF16); BVc = Buf()
    wring = Rot([(sb("wr%d" % i, [128, WSLOT], BF16), Buf("wr%d" % i)) for i in range(NW)])
    xst = Rot([(sb("xst%d" % i, [128, KC * NS], F32), Buf()) for i in range(2)])
    sqb = sb("sqb", [128, KC * NS], BF16); Bsqb = Buf()
    rsb = Rot([(sb("rsb%d" % i, [128, NS], F32), Buf()) for i in range(2)])
    BIG = 82 * 1024
    big = sb("big", [128, BIG // 2], BF16)
    tmpf_items = [(sb("tmpf%d" % i, [128, 520], F32), Buf()) for i in range(7)]
    tmpf = Rot(tmpf_items)
    tmpC = Rot(tmpf_items[0:2])
    tmpG = Rot(tmpf_items[4:7])
    xres = Rot(tmpf_items[0:2])
    xoutb = Rot(tmpf_items[2:4])
    pT = Rot([(sb("pT%d" % i, [128, 512], BF16), Buf()) for i in range(14)])
    smallf = sb("smallf", [128, 64], F32); Bsmall = Buf()
    ps = [nc.alloc_psum_tensor("ps%d" % i, [128, 512], F32) for i in range(8)]
    Bps = [Buf("ps%d" % i) for i in range(8)]
    psAll = Rot([(ps[i], Bps[i]) for i in range(8)])
    psS = Rot([(ps[i], Bps[i]) for i in range(3)])
    psC = Rot([(ps[i], Bps[i]) for i in range(4)])
    psO = Rot([(ps[i], Bps[i]) for i in (4, 5)])
    rcb = Rot([(sb("rcb%d" % i, [64, 512], F32), Buf()) for i in range(1)])

    def bigv(off_bytes, nbytes, dt):
        if dt == BF16:
            return big[:, off_bytes // 2:(off_bytes + nbytes) // 2]
        return big[:, off_bytes // 2:(off_bytes + nbytes) // 2].bitcast(F32)

    trst = Rot([(bigv(i * 8192, 8192, F32), Buf()) for i in range(2)])
    ckst = bigv(0, 4096, F32); Bckst = Buf()
    wsTf = bigv(4096, 2048, F32)
    o = 0
    h1 = bigv(o, KC * 768 * 2, BF16); o += KC * 768 * 2
    kT = bigv(o, 2 * 768 * 2, BF16); o += 2 * 768 * 2
    Vx = bigv(o, 6 * 4 * 128 * 2, BF16); o += 6 * 4 * 128 * 2
    qT = bigv(o, 8 * 512 * 2, BF16); o += 8 * 512 * 2
    cacc = bigv(o, 4 * 512 * 4, F32); o += 4 * 512 * 4
    mixT = bigv(o, KC * 512 * 2, BF16); o += KC * 512 * 2
    gu = bigv(o, 4 * 512 * 2, BF16); o += 4 * 512 * 2
    nTok = bigv(o, 4 * 512 * 2, BF16); o += 4 * 512 * 2
    glub = Rot([(bigv(o + i * 544 * 2, 544 * 2, BF16), Buf()) for i in range(2)]); o += 2 * 544 * 2
    ropC = bigv(o, 768 * 4, F32); o += 768 * 4
    ropS = bigv(o, 768 * 4, F32); o += 768 * 4
    assert o <= BIG, o
    BkT, BVx, BqT, Bgu, Brop = [Buf() for _ in range(5)]
    BnTok = [Buf() for _ in range(4)]
    Bh1 = [Buf() for _ in range(KC)]
    BmixT = [Buf() for _ in range(KC)]
    Bcacc = [Buf() for _ in range(4)]
    o = 0
    h2 = bigv(o, KC * (TF + 4) * 2, BF16); o += KC * (TF + 4) * 2
    actT = bigv(o, NJ * TF * 2, BF16); o += NJ * TF * 2
    assert o <= BIG, o
    Bh2 = [Buf() for _ in range(KC)]
    Bact = [Buf() for _ in range(NJ)]

    h1v = v3(h1, KC)
    kTv = v3(kT, 2)
    Vxv = Vx.rearrange("p (b k d) -> p b k d", b=6, k=4)
    qTv = v3(qT, 8)
    mixTv = v3(mixT, KC)
    guv = v3(gu, 4)
    nTokv = nTok.rearrange("p (b g c) -> p b g c", b=4, g=4)
    caccv = v3(cacc, 4)
    h2v = v3(h2, KC)
    actv = v3(actT, NJ)
    Vcv = Vc[:, :].rearrange("p (b k d) -> p b k d", b=4, k=4)
    KcTv = v3(KcT[:, :], 2)

    def pvc(l, off, n=1):
        b = l * PV_L + off
        return pv[:, b:b + n]

    def wload(src, ncols, kc=KC):
        t, B = wring.get()
        fw.dma("pool", v3(t[:, 0:kc * ncols], kc), src, writes=[B])
        return v3(t[:, 0:kc * ncols], kc), B

    def w_cols(wd, l, c0, w):
        return wd[l].rearrange("(kc p) n -> p kc n", p=128)[:, :, c0:c0 + w]

    def norm_L(X, BX, tok0, n):
        st, Bst = xst.get()
        st3 = v3(st[:, :], KC)
        fw.dma("sp", st3[:, :, 0:n], X[:, :, tok0:tok0 + n], reads=blkbufs(BX, tok0, n), writes=[Bst])
        return (st3, Bst, n)

    def norm_A(X, BX, tok0, n, pre=None):
        st3, Bst, n = pre if pre is not None else norm_L(X, BX, tok0, n)
        sq3 = v3(sqb[:, :], KC)
        fw.op("act", "activation", out=sq3[:, :, 0:n], in_=st3[:, :, 0:n], func=AF.Square, reads=[Bst], writes=[Bsqb])
        p, Bp = psAll.get()
        for c in range(KC):
            fw.op("pe", "matmul", out=p[:, 0:n], lhsT=ones_bf[:, :], rhs=sq3[:, c, 0:n], start=(c == 0), stop=(c == KC - 1),
                  reads=[Bsqb, Bones], writes=[Bp], inc=(c == KC - 1))
        return (st3, Bst, p, Bp, n)

    def norm_B(ctx, l, r, which, outfn, Bout):
        st3, Bst, p, Bp, n = ctx
        rs, Brs = rsb.get()
        fw.op("act", "activation", out=rs[:, 0:n], in_=p[:, 0:n], func=AF.Sqrt, scale=1.0 / D, bias=epsT[:, 0:1], reads=[Bp], writes=[Brs])
        fw.op("dve", "reciprocal", out=rs[:, 0:n], in_=rs[:, 0:n], reads=[Brs], writes=[Brs])
        fw.op("dve", "tensor_tensor", out=st3[:, :, 0:n], in0=st3[:, :, 0:n], in1=rs[:, 0:n].unsqueeze(1).broadcast_to([128, KC, n]),
              op=ALU.mult, reads=[Bst, Brs], writes=[Bst])
        for c in range(KC):
            if which == 2:
                fw.op("dve", "tensor_scalar", out=outfn(c), in0=st3[:, c, 0:n], scalar1=pv[:, 2 * PV_L + c:2 * PV_L + c + 1], scalar2=None,
                      op0=ALU.mult, reads=[Bst, Bpv], writes=[Bout[c] if isinstance(Bout, list) else Bout])
                continue
            A = der[l][:, r * 32 + which * 16 + c:r * 32 + which * 16 + c + 1]
            Bsh = mod[l][:, ((0 if which == 0 else 48) + c) * 2 + r:((0 if which == 0 else 48) + c) * 2 + r + 1]
            if c % 2 == 0:
                fw.op("act", "activation", out=outfn(c), in_=st3[:, c, 0:n], func=AF.Identity, scale=A, bias=Bsh,
                      reads=[Bst, Bder[l], Bmod[l]], writes=[Bout[c] if isinstance(Bout, list) else Bout])
            else:
                fw.op(NORM_ODD_ENG, "tensor_scalar", out=outfn(c), in0=st3[:, c, 0:n], scalar1=A, scalar2=Bsh, op0=ALU.mult, op1=ALU.add,
                      reads=[Bst, Bder[l], Bmod[l]], writes=[Bout[c] if isinstance(Bout, list) else Bout])

    def rmsnorm(X, BX, tok0, n, l, r, which, outfn, Bout):
        norm_B(norm_A(X, BX, tok0, n), l, r, which, outfn, Bout)

    def norm_steps(X, BX, subs, l, r, which, Bout):
        pre = [norm_L(X, BX, subs[j][0], subs[j][1]) for j in range(min(2, len(subs)))]
        yield "P"
        ctx = norm_A(X, BX, subs[0][0], subs[0][1], pre[0])
        yield
        for j in range(len(subs)):
            nxt = None
            if j + 1 < len(subs):
                nxt = norm_A(X, BX, subs[j + 1][0], subs[j + 1][1], pre[1] if j == 0 else None)
                yield
            norm_B(ctx, l, r, which, subs[j][2], Bout)
            yield
            ctx = nxt

    def proj(wt, Bw, wcol0, act3, Bact_, slices, kc=KC, rot=None, extra_reads=()):
        outs = []
        for (c0, n) in slices:
            p, Bp = (rot or psAll).get()
            outs.append((p, Bp, n))
        for k in range(kc):
            for si, (c0, n) in enumerate(slices):
                p, Bp, _ = outs[si]
                last = (k == kc - 1)
                fw.op("pe", "matmul", out=p[:, 0:n], lhsT=wt[:, k, wcol0:wcol0 + 128], rhs=act3[:, k, c0:c0 + n],
                      start=(k == 0), stop=last, reads=[Bw, (Bact_[k] if isinstance(Bact_, list) else Bact_)] + list(extra_reads), writes=[Bp], inc=last)
        return outs

    fw.dma("sp", pv[:, :], pvec_d, writes=[Bpv])
    fw.dma("sp", ident[:, :], ident_d, writes=[Bident])
    fw.dma("sp", perm[:, :], perm_d, writes=[Bperm])
    fw.dma("sp", ccs[:, :], ccT_d, writes=[Bccs])
    fw.dma("pool", maskP[:, :], maskP_d[:, 0:128], writes=[BmaskP])
    fw.dma("pool", maskN[:, :], maskN_d[:, 0:128], writes=[BmaskN])
    fw.op("dve", "tensor_copy", out=ident_bf[:, :], in_=ident[:, :], reads=[Bident], writes=[Bident])
    fw.op("dve", "memset", ap=ones_bf[:, :], constant=1.0, writes=[Bones])
    fw.op("dve", "memset", ap=ones_f[:, :], constant=1.0, writes=[Bones])
    fw.op("dve", "memset", ap=zo[:, 0:64], constant=0.0, writes=[Bones])
    fw.op("dve", "memset", ap=zo[:, 64:128], constant=1.0, writes=[Bones])
    fw.op("dve", "memset", ap=epsT[:, :], constant=EPS, writes=[Bones])
    fw.op("dve", "memset", ap=Vc[:, :], constant=1.0, writes=[BVc])
    fw.op("act", "activation", out=scT[:, :], in_=ccs[:, :], func=AF.Silu, reads=[Bccs], writes=[BscT])

    def ada_gen():
        for l in range(2):
            pm, Bpm = ps[7 - l], Bps[7 - l]
            for cg in range(48):
                wt, Bw = wload(w_cols(w_ada_d, l, cg * 256, 256), 256)
                for oc in range(2):
                    j = cg * 2 + oc
                    for kc in range(KC):
                        fw.op("pe", "matmul", out=pm[:, 2 * j:2 * j + 2], lhsT=wt[:, kc, oc * 128:(oc + 1) * 128], rhs=scT[:, 2 * kc:2 * kc + 2],
                              start=(kc == 0), stop=(kc == KC - 1), reads=[Bw, BscT], writes=[Bpm], inc=(kc == KC - 1))
                yield
            fw.op("dve", "tensor_tensor", out=mod[l][:, :].rearrange("p (j r) -> p j r", r=2), in0=pm[:, 0:192].rearrange("p (j r) -> p j r", r=2),
                  in1=pvc(l, 32, 96).unsqueeze(2).broadcast_to([128, 96, 2]), op=ALU.add, reads=[Bpm, Bpv], writes=[Bmod[l]])
            m3 = mod[l][:, :].rearrange("p (j r) -> p j r", r=2)
            for r in range(2):
                for which in range(2):
                    sc = m3[:, (16 if which == 0 else 64):(32 if which == 0 else 80), r]
                    dst = der[l][:, r * 32 + which * 16:r * 32 + which * 16 + 16]
                    fw.op("dve", "tensor_scalar", out=dst, in0=sc, scalar1=1.0, scalar2=None, op0=ALU.add, reads=[Bmod[l]], writes=[Bder[l]])
                    fw.op("dve", "tensor_tensor", out=dst, in0=dst, in1=pvc(l, 0 if which == 0 else 16, 16), op=ALU.mult,
                          reads=[Bder[l], Bpv], writes=[Bder[l]])
            yield

    psI = Rot([(ps[i], Bps[i]) for i in range(6)])

    def init_gen():
        for blk in range(NT // 128):
            t0 = blk * 128
            src = xs_d[t0:t0 + 128, :] if t0 < LS else xp_d[t0 - LS:t0 - LS + 128, :]
            st, Bst = trst.get()
            fw.dma("sp", st[:, :], src, writes=[Bst])
            so, Bso = xst.get()
            so3 = v3(so[:, 0:KC * 128], KC)
            for b4 in range(4):
                p, Bp = psI.get()
                for q in range(4):
                    c = b4 * 4 + q
                    fw.op("pe", "transpose", out=p[:, q * 128:(q + 1) * 128], in_=st[:, c * 128:(c + 1) * 128], identity=ident[:, :],
                          reads=[Bst, Bident], writes=[Bp], inc=(q == 3))
                if b4 % 2 == 0:
                    fw.op("act", "copy", out=so[:, b4 * 512:(b4 + 1) * 512], in_=p[:, :], reads=[Bp], writes=[Bso])
                else:
                    fw.op("dve", "tensor_copy", out=so[:, b4 * 512:(b4 + 1) * 512], in_=p[:, :], reads=[Bp], writes=[Bso])
            fw.dma("sp", XA[:, :, t0:t0 + 128], so3, reads=[Bso], writes=blkbufs(BXA, t0, 128))
            yield

    ga_, gi_ = ada_gen(), init_gen()
    a_alive = i_alive = True
    while a_alive or i_alive:
        for _ in range(3):
            if a_alive:
                try:
                    next(ga_)
                except StopIteration:
                    a_alive = False
        if i_alive:
            try:
                next(gi_)
            except StopIteration:
                i_alive = False

    def layer_setup(l):
        prow, Bprow = bigv(8192, 4096, F32), Buf()
        fw.dma("sp", prow[0:1, 0:NPR], prow_d[l:l + 1, :], writes=[Bprow])
        for g in range(4):
            st, Bst = tmpf.get()
            fw.dma("sp", st[:, 0:128], ws_d[l, g], writes=[Bst])
            p, Bp = psAll.get()
            fw.op("pe", "transpose", out=p[:, 0:128], in_=st[:, 0:128], identity=ident[:, :], reads=[Bst, Bident], writes=[Bp])
            fw.op("act", "copy", out=wsT[:, g * 128:(g + 1) * 128], in_=p[:, 0:128], reads=[Bp], writes=[BwsT])
            fw.op("dve", "tensor_copy", out=wsTf[:, g * 128:(g + 1) * 128], in_=p[:, 0:128], reads=[Bp], writes=[BwsT])
            p2, Bp2 = psAll.get()
            fw.op("pe", "matmul", out=p2[:, 0:128], lhsT=ones_f[:, :], rhs=wsTf[:, g * 128:(g + 1) * 128], start=True, stop=True,
                  reads=[BwsT, Bones], writes=[Bp2])
            p3, Bp3 = psAll.get()
            fw.op("pe", "matmul", out=p3[:, 0:128], lhsT=ones_f[0:1, :], rhs=prow[0:1, g * 128:(g + 1) * 128],
                  start=True, stop=True, reads=[Bprow, Bones], writes=[Bp3])
            t2, Bt2 = tmpf.get()
            fw.op("act", "copy", out=t2[:, 0:128], in_=p3[:, 0:128], reads=[Bp3], writes=[Bt2])
            fw.op("dve", "scalar_tensor_tensor", out=Bmat[:, g * 128:(g + 1) * 128], in0=p2[:, 0:128], scalar=pvc(l, 620 + g), in1=t2[:, 0:128],
                  op0=ALU.mult, op1=ALU.add, reads=[Bp2, Bt2, Bpv], writes=[BBmat])
        fw.op("act", "activation", out=sinkf[:, 0:16], in_=prow[0:1, 512:528], func=AF.Exp, reads=[Bprow], writes=[Bsink])
        fw.op("dve", "tensor_copy", out=sinkhl[:, 0:16], in_=sinkf[:, 0:16], reads=[Bsink], writes=[Bsink])
        fw.op("dve", "tensor_copy", out=sinkf[:, 16:32], in_=sinkhl[:, 0:16], reads=[Bsink], writes=[Bsink])
        fw.op("dve", "tensor_tensor", out=sinkf[:, 32:48], in0=sinkf[:, 0:16], in1=sinkf[:, 16:32], op=ALU.subtract, reads=[Bsink], writes=[Bsink])
        fw.op("dve", "tensor_copy", out=sinkhl[:, 16:32], in_=sinkf[:, 32:48], reads=[Bsink], writes=[Bsink])
        fw.dma("sp", ckst[:, :].rearrange("p (b f) -> p b f", b=4), ck_d[l].rearrange("(b p) f -> p b f", p=128), writes=[Bckst])
        for b in range(4):
            for ch in range(2):
                p, Bp = psAll.get()
                fw.op("pe", "transpose", out=p[:, 0:128], in_=ckst[:, b * 256 + ch * 128:b * 256 + (ch + 1) * 128], identity=ident[:, :],
                      reads=[Bckst, Bident], writes=[Bp])
                fw.op("act", "copy", out=KcTv[:, ch, b * 128:(b + 1) * 128], in_=p[:, 0:128], reads=[Bp], writes=[BKcT])
        for b in range(4):
            fw.dma("pool", Vcv[:, b, :, 0:64], cv_d[l, b * 128:(b + 1) * 128, :].rearrange("p (k d) -> p k d", k=4), writes=[BVc])

    def mixer_tile(l, kind, s, T, seq0, L, pr):
        r = kind
        sample = (kind == 0)
        e = s + T
        has_l = s > 0
        has_r = e < L
        lo = s - 128 if has_l else s
        hi = e + 128 if has_r else e
        col = lambda t: t - (s - 128)
        nb = T // 128
        subs = []
        t = lo
        while t < hi:
            n = min(128, hi - t)
            subs.append((seq0 + t, n, (lambda c, t=t, n=n: h1v[:, c, col(t):col(t) + n])))
            t += n
        for tk in norm_steps(XA, BXA, subs, l, r, 0, Bh1):
            yield tk or "N"
        yield "ENDNORM"
        if sample:
            fw.dma("sp", ropC[:, col(lo):col(hi)], ropeC_d[:, lo:hi], writes=[Brop])
            fw.dma("sp", ropS[:, col(lo):col(hi)], ropeS_d[:, lo:hi], writes=[Brop])

        def rope_evac(p, Bp, n, c0, dst, Bdst):
            kf, Bkf = tmpf.get()
            fw.op("act", "copy", out=kf[:, 0:n], in_=p[:, 0:n], reads=[Bp], writes=[Bkf])
            p2, Bp2 = psAll.get()
            fw.op("pe", "matmul", out=p2[:, 0:n], lhsT=perm[:, :], rhs=kf[:, 0:n], start=True, stop=True, reads=[Bkf, Bperm], writes=[Bp2])
            t1, Bt1 = tmpf.get()
            fw.op("dve", "tensor_tensor", out=t1[:, 0:n], in0=kf[:, 0:n], in1=ropC[:, c0:c0 + n], op=ALU.mult, reads=[Bkf, Brop], writes=[Bt1])
            t2, Bt2 = tmpf.get()
            fw.op("dve", "tensor_tensor", out=t2[:, 0:n], in0=p2[:, 0:n], in1=ropS[:, c0:c0 + n], op=ALU.mult, reads=[Bp2, Brop], writes=[Bt2])
            fw.op("dve", "tensor_tensor", out=dst, in0=t1[:, 0:n], in1=t2[:, 0:n], op=ALU.add, reads=[Bt1, Bt2], writes=[Bdst])

        wk, Bwk = wload(w_cols(w_in_d, l, 1024, 256), 256)
        wv, Bwv = wload(w_cols(w_in_d, l, 1280, 256), 256)
        nall = hi - lo
        kslices = []
        t = lo
        while t < hi:
            n = min(384, hi - t)
            kslices.append((col(t), n))
            t += n
        for ch in range(2):
            outs = proj(wk, Bwk, ch * 128, h1v, Bh1, kslices)
            for (p, Bp, n), (c0, _) in zip(outs, kslices):
                if sample:
                    rope_evac(p, Bp, n, c0, kTv[:, ch, c0:c0 + n], BkT)
                else:
                    fw.op("act", "copy", out=kTv[:, ch, c0:c0 + n], in_=p[:, 0:n], reads=[Bp], writes=[BkT])
        for b in range((hi - lo) // 128):
            c0 = col(lo) + b * 128
            bi = c0 // 128
            p, Bp = psAll.get()
            for k in range(KC):
                fw.op("pe", "matmul", out=p[:, 0:256], lhsT=h1v[:, k, c0:c0 + 128], rhs=wv[:, k, 0:256], start=(k == 0), stop=(k == KC - 1),
                      reads=[Bh1[k], Bwv], writes=[Bp], inc=(k == KC - 1))
            fw.op("act", "copy", out=Vxv[:, bi, :, 0:64], in_=p[:, 0:256].rearrange("p (k d) -> p k d", k=4), reads=[Bp], writes=[BVx])
            fw.op("dve", "memset", ap=Vxv[:, bi, :, 64:128], constant=1.0, writes=[BVx])
            if not sample:
                to, Bto = xoutb.get()
                fw.op("dve", "tensor_copy", out=to[:, 0:256], in_=p[:, 0:256], reads=[Bp], writes=[Bto])
                fw.dma("sp", nv_d[pr, l, s + b * 128:s + (b + 1) * 128, :], to[:, 0:256], reads=[Bto], writes=[Buf()])
                p2, Bp2 = psAll.get()
                for k in range(KC):
                    fw.op("pe", "matmul", out=p2[:, 0:256], lhsT=h1v[:, k, c0:c0 + 128], rhs=wk[:, k, 0:256], start=(k == 0), stop=(k == KC - 1),
                          reads=[Bh1[k], Bwk], writes=[Bp2], inc=(k == KC - 1))
                to2, Bto2 = xoutb.get()
                fw.op("act", "copy", out=to2[:, 0:256], in_=p2[:, 0:256], reads=[Bp2], writes=[Bto2])
                fw.dma("sp", nk_d[pr, l, s + b * 128:s + (b + 1) * 128, :], to2[:, 0:256], reads=[Bto2], writes=[Buf()])
        cs = col(s)
        for half in range(4):
            wq, Bwq = wload(w_cols(w_in_d, l, half * 256, 256), 256)
            for cc in range(2):
                ch = half * 2 + cc
                (p, Bp, n), = proj(wq, Bwq, cc * 128, h1v, Bh1, [(cs, T)])
                if sample:
                    rope_evac(p, Bp, T, cs, qTv[:, ch, 0:T], BqT)
                else:
                    fw.op("act", "copy", out=qTv[:, ch, 0:T], in_=p[:, 0:T], reads=[Bp], writes=[BqT])
        def attention_gen():
            groups = [(qb, kv) for qb in range(nb) for kv in range(4)]

            def stage1(qb, kv):
                plo = (kv % 2) * 64
                kch = kv // 2
                qc0 = (kv // 2) * 4
                keys = []
                if sample:
                    gb = (s // 128) + qb
                    for dlt in (-1, 0, 1):
                        g2 = gb + dlt
                        if g2 < 0 or g2 >= L // 128:
                            continue
                        c0 = col(g2 * 128)
                        keys.append((kTv[plo:plo + 64, kch, c0:c0 + 128], Vxv[:, c0 // 128, kv, :], dlt, [BkT], [BVx]))
                    for b in range(4):
                        keys.append((KcTv[plo:plo + 64, kch, b * 128:(b + 1) * 128], Vcv[:, b, kv, :], 0, [BKcT], [BVc]))
                else:
                    for b in range(L // 128):
                        c0 = col(b * 128)
                        keys.append((kTv[plo:plo + 64, kch, c0:c0 + 128], Vxv[:, c0 // 128, kv, :], 0, [BkT], [BVx]))
                rhs_q = qTv[plo:plo + 64, qc0:qc0 + 4, qb * 128:(qb + 1) * 128]
                pts = []
                for (kap, vap, dlt, kr, vr) in keys:
                    p, Bp = psS.get()
                    fw.op("pe", "matmul", out=p[:, :], lhsT=kap, rhs=rhs_q, start=True, stop=True, reads=kr + [BqT], writes=[Bp])
                    pt, Bpt = pT.get()
                    fw.op("act", "activation", out=pt[:, :], in_=p[:, :], func=AF.Exp, scale=SCALE, reads=[Bp], writes=[Bpt])
                    if dlt != 0:
                        mk, Bmk = (maskP, BmaskP) if dlt == -1 else (maskN, BmaskN)
                        fw.op("dve", "tensor_tensor", out=pt[:, :].rearrange("p (g t) -> p g t", g=4), in0=pt[:, :].rearrange("p (g t) -> p g t", g=4),
                              in1=mk[:, :].unsqueeze(1).broadcast_to([128, 4, 128]), op=ALU.mult, reads=[Bpt, Bmk], writes=[Bpt])
                    pts.append((pt, Bpt, vap, vr))
                return pts

            def stage2(qb, kv, pts):
                po, Bpo = psO.get()
                for i, (pt, Bpt, vap, vr) in enumerate(pts):
                    fw.op("pe", "matmul", out=po[:, :], lhsT=vap, rhs=pt[:, :], start=(i == 0), stop=False, reads=vr + [Bpt], writes=[Bpo], inc=False)
                for hl in range(2):
                    fw.op("pe", "matmul", out=po[:, :], lhsT=zo[0:1, :],
                          rhs=sinkhl[0:1, hl * 16 + kv * 4:hl * 16 + kv * 4 + 4].unsqueeze(2).broadcast_to([1, 4, 128]),
                          start=False, stop=(hl == 1), reads=[Bsink, Bones], writes=[Bpo], inc=(hl == 1))
                rc, Brc = rcb.get()
                fw.op("dve", "reciprocal", out=rc[0:64, 0:512], in_=po[64:128, :], reads=[Bpo], writes=[Brc])
                po4 = po[0:64, :].rearrange("p (a b t) -> p a b t", a=2, b=2)
                rc4 = rc[0:64, 0:512].rearrange("p (a b t) -> p a b t", a=2, b=2)
                for par in range(2):
                    fw.op("dve", "tensor_tensor", out=mixTv[par * 64:par * 64 + 64, kv * 2:kv * 2 + 2, qb * 128:(qb + 1) * 128],
                          in0=po4[:, :, par, :], in1=rc4[:, :, par, :], op=ALU.mult, reads=[Bpo, Brc], writes=BmixT[kv * 2:kv * 2 + 2])

            cur = stage1(*groups[0])
            yield
            for gi, (qb, kv) in enumerate(groups):
                nxt = None
                if gi + 1 < len(groups):
                    nxt = stage1(*groups[gi + 1])
                    yield
                stage2(qb, kv, cur)
                yield
                cur = nxt

        def conv_gen():
            clo = max(s - 15, 0) if has_l else s
            chi = min(e + 15, L) if has_r else e
            gcol = lambda t: t - (s - 15)
            cslices = []
            t = clo
            ncs = (chi - clo + 511) // 512
            for i in range(ncs):
                n = (chi - t + (ncs - i) - 1) // (ncs - i)
                cslices.append((col(t), n, gcol(t)))
                t += n
            wa2 = [wload(w_cols(w_in_d, l, 1536 + i * 256, 256), 256) for i in range(2)]
            wg2 = [wload(w_cols(w_in_d, l, 2048 + i * 256, 256), 256) for i in range(2)]
            p1, Bp1 = ps[6], Bps[6]
            p2, Bp2 = ps[7], Bps[7]
            for c in range(4):
                gl, Bglu = glub.get()
                if not has_l:
                    fw.op("dve", "memset", ap=gl[:, 0:15], constant=0.0, writes=[Bglu])
                if not has_r:
                    fw.op("dve", "memset", ap=gl[:, 15 + T:30 + T], constant=0.0, writes=[Bglu])
                (wa, Bwa), (wg, Bwg) = wa2[c // 2], wg2[c // 2]
                oa = proj(wa, Bwa, (c % 2) * 128, h1v, Bh1, [(c0, n) for (c0, n, _) in cslices], rot=psC)
                og = proj(wg, Bwg, (c % 2) * 128, h1v, Bh1, [(c0, n) for (c0, n, _) in cslices], rot=psC)
                for (pa, Bpa, n), (pg, Bpg, _), (_, _, g0) in zip(oa, og, cslices):
                    sg, Bsg = tmpC.get()
                    fw.op("act", "activation", out=sg[:, 0:n], in_=pg[:, 0:n], func=AF.Sigmoid, reads=[Bpg], writes=[Bsg])
                    fw.op("dve", "tensor_tensor", out=gl[:, g0:g0 + n], in0=pa[:, 0:n], in1=sg[:, 0:n], op=ALU.mult, reads=[Bpa, Bsg], writes=[Bglu])
                yield
                pc, Bpc = ps[3], Bps[3]
                for k0 in range(0, 31, 8):
                    dg, Bdg = dgb.get()
                    kn = min(8, 31 - k0)
                    for j in range(kn):
                        if j % 2 == 0:
                            fw.op("act", "activation", out=dg[:, j * 128:(j + 1) * 128], in_=ident_bf[:, :], func=AF.Copy, scale=pvc(l, 480 + (k0 + j) * 4 + c),
                                  reads=[Bident, Bpv], writes=[Bdg[j]])
                        else:
                            fw.op("dve", "tensor_scalar", out=dg[:, j * 128:(j + 1) * 128], in0=ident_bf[:, :], scalar1=pvc(l, 480 + (k0 + j) * 4 + c), scalar2=None,
                                  op0=ALU.mult, reads=[Bident, Bpv], writes=[Bdg[j]])
                    for j in range(kn):
                        k = k0 + j
                        fw.op("pe", "matmul", out=pc[:, 0:T], lhsT=dg[:, j * 128:(j + 1) * 128], rhs=gl[:, k:k + T], start=(k == 0), stop=(k == 30),
                              reads=[Bdg[j], Bglu], writes=[Bpc], inc=(j == kn - 1))
                    yield
                fw.op("act", "activation", out=caccv[:, c, 0:T], in_=pc[:, 0:T], func=AF.Identity, scale=1.0, bias=pvc(l, 604 + c),
                      reads=[Bpc, Bpv], writes=[Bcacc[c]])
                sq, Bsq = tmpC.get()
                fw.op("act", "activation", out=sq[:, 0:T], in_=caccv[:, c, 0:T], func=AF.Square, reads=[Bcacc[c]], writes=[Bsq])
                fw.op("pe", "matmul", out=p1[:, 0:T], lhsT=ones_f[:, :], rhs=caccv[:, c, 0:T], start=(c == 0), stop=(c == 3), reads=[Bcacc[c], Bones],
                      writes=[Bp1])
                fw.op("pe", "matmul", out=p2[:, 0:T], lhsT=ones_f[:, :], rhs=sq[:, 0:T], start=(c == 0), stop=(c == 3), reads=[Bsq, Bones],
                      writes=[Bp2])
                yield
            mu, Bmu = tmpf_items[2]
            fw.op("act", "activation", out=mu[:, 0:T], in_=p1[:, 0:T], func=AF.Identity, scale=1.0 / 512, reads=[Bp1], writes=[Bmu])
            msq, Bmsq = tmpf_items[3]
            fw.op("dve", "tensor_tensor", out=msq[:, 0:T], in0=mu[:, 0:T], in1=mu[:, 0:T], op=ALU.mult, reads=[Bmu], writes=[Bmsq])
            fw.op("dve", "scalar_tensor_tensor", out=msq[:, 0:T], in0=p2[:, 0:T], scalar=1.0 / 512, in1=msq[:, 0:T], op0=ALU.mult, op1=ALU.subtract,
                  reads=[Bp2, Bmsq], writes=[Bmsq])
            fw.op("act", "activation", out=msq[:, 0:T], in_=msq[:, 0:T], func=AF.Sqrt, scale=1.0, bias=epsT[:, 0:1], reads=[Bmsq], writes=[Bmsq])
            fw.op("dve", "reciprocal", out=msq[:, 0:T], in_=msq[:, 0:T], reads=[Bmsq], writes=[Bmsq])
            yield
            for c in range(4):
                fw.op("dve", "tensor_tensor", out=caccv[:, c, 0:T], in0=caccv[:, c, 0:T], in1=mu[:, 0:T], op=ALU.subtract, reads=[Bcacc[c], Bmu], writes=[Bcacc[c]])
                fw.op("dve", "tensor_tensor", out=caccv[:, c, 0:T], in0=caccv[:, c, 0:T], in1=msq[:, 0:T], op=ALU.mult, reads=[Bcacc[c], Bmsq], writes=[Bcacc[c]])
                fw.op("act", "activation", out=mixTv[:, 8 + c, 0:T], in_=caccv[:, c, 0:T], func=AF.Silu, scale=pvc(l, 608 + c), bias=pvc(l, 612 + c),
                      reads=[Bcacc[c], Bpv], writes=[BmixT[8 + c]])
                yield

        def gmlp_gen():
            wgu2 = [wload(w_cols(w_in_d, l, 2560 + i * 256, 256), 256) for i in range(2)]
            wgv2 = [wload(w_cols(w_in_d, l, 3072 + i * 256, 256), 256) for i in range(2)]
            for c in range(4):
                wgu, Bwgu = wgu2[c // 2]
                (p, Bp, n), = proj(wgu, Bwgu, (c % 2) * 128, h1v, Bh1, [(cs, T)], rot=psS)
                fw.op("act", "activation", out=guv[:, c, 0:T], in_=p[:, 0:T], func=AF.Gelu_apprx_tanh, reads=[Bp], writes=[Bgu])
                yield
            for b in range(nb):
                c0 = cs + b * 128
                p, Bp = psS.get()
                for hv in range(2):
                    wgv, Bwgv = wgv2[hv]
                    for k in range(KC):
                        fw.op("pe", "matmul", out=p[:, hv * 256:(hv + 1) * 256], lhsT=h1v[:, k, c0:c0 + 128], rhs=wgv[:, k, :], start=(k == 0),
                              stop=(k == KC - 1), reads=[Bh1[k], Bwgv], writes=[Bp], inc=(k == KC - 1))
                gvt, Bgvt = tmpG.get()
                fw.op("act", "activation", out=gvt[:, 0:512], in_=p[:, :], func=AF.Gelu_apprx_tanh, reads=[Bp], writes=[Bgvt])
                g2, Bg2 = tmpG.get()
                fw.op("act", "activation", out=g2[:, 0:512], in_=gvt[:, 0:512], func=AF.Square, reads=[Bgvt], writes=[Bg2])
                yield
                sm = smallf
                fw.op("dve", "reduce_sum", out=sm[:, 0:4], in_=gvt[:, 0:512].rearrange("p (g c) -> p g c", g=4), axis=AX.X, reads=[Bgvt], writes=[Bsmall])
                fw.op("dve", "reduce_sum", out=sm[:, 4:8], in_=g2[:, 0:512].rearrange("p (g c) -> p g c", g=4), axis=AX.X, reads=[Bg2], writes=[Bsmall])
                fw.op("dve", "tensor_scalar", out=sm[:, 0:8], in0=sm[:, 0:8], scalar1=1.0 / 128, scalar2=None, op0=ALU.mult, reads=[Bsmall], writes=[Bsmall])
                fw.op("dve", "tensor_tensor", out=sm[:, 8:12], in0=sm[:, 0:4], in1=sm[:, 0:4], op=ALU.mult, reads=[Bsmall], writes=[Bsmall])
                fw.op("dve", "tensor_tensor", out=sm[:, 8:12], in0=sm[:, 4:8], in1=sm[:, 8:12], op=ALU.subtract, reads=[Bsmall], writes=[Bsmall])
                fw.op("act", "activation", out=sm[:, 8:12], in_=sm[:, 8:12], func=AF.Sqrt, scale=1.0, bias=epsT[:, 0:1], reads=[Bsmall], writes=[Bsmall])
                fw.op("dve", "reciprocal", out=sm[:, 8:12], in_=sm[:, 8:12], reads=[Bsmall], writes=[Bsmall])
                sp_, Bsp = psS.get()
                for g in range(4):
                    fw.op("dve", "tensor_scalar", out=nTokv[:, b, g, :], in0=gvt[:, g * 128:(g + 1) * 128], scalar1=sm[:, g:g + 1], scalar2=sm[:, 8 + g:9 + g],
                          op0=ALU.subtract, op1=ALU.mult, reads=[Bgvt, Bsmall], writes=[BnTok[g]])
                for g in range(4):
                    fw.op("pe", "matmul", out=sp_[:, g * 128:(g + 1) * 128], lhsT=nTokv[:, b, g, :], rhs=wsT[:, g * 128:(g + 1) * 128], start=True, stop=True,
                          reads=[BnTok[g], BwsT], writes=[Bsp])
                sv, Bsv = tmpG.get()
                for g in range(4):
                    fw.op("dve", "scalar_tensor_tensor", out=sv[:, g * 128:(g + 1) * 128], in0=sp_[:, g * 128:(g + 1) * 128], scalar=pvc(l, 616 + g),
                          in1=Bmat[:, g * 128:(g + 1) * 128], op0=ALU.mult, op1=ALU.add, reads=[Bsp, BBmat, Bpv], writes=[Bsv])
                fw.op("dve", "tensor_tensor", out=mixTv[:, 12:16, b * 128:(b + 1) * 128], in0=guv[:, :, b * 128:(b + 1) * 128],
                      in1=sv[:, 0:512].rearrange("p (g t) -> p g t", g=4), op=ALU.mult, reads=[Bgu, Bsv], writes=BmixT[12:16])
                yield

        def step(g_):
            try:
                next(g_)
                return True
            except StopIteration:
                return False

        ga, gg, gc = attention_gen(), gmlp_gen(), conv_gen()
        a_alive = True
        while step(gg):
            if a_alive:
                a_alive = step(ga)
        c_alive = True
        while a_alive or c_alive:
            if a_alive:
                a_alive = step(ga)
            if c_alive:
                c_alive = step(gc)
        yield "PREOUT"
        for og in range(8):
            wo, Bwo = wload(w_cols(w_out_d, l, og * 256, 256), 256)
            for cc in range(2):
                oc = og * 2 + cc
                xr, Bxr = xres.get()
                fw.dma("sp", xr[:, 0:T], XA[:, oc, seq0 + s:seq0 + e], reads=blkbufs(BXA, seq0 + s, T, oc), writes=[Bxr])
                (p, Bp, n), = proj(wo, Bwo, cc * 128, mixTv, BmixT, [(0, T)])
                xo, Bxo = xoutb.get()
                fw.op("dve", "scalar_tensor_tensor", out=xo[:, 0:T], in0=p[:, 0:T], scalar=mod[l][:, (32 + oc) * 2 + r:(32 + oc) * 2 + r + 1],
                      in1=xr[:, 0:T], op0=ALU.mult, op1=ALU.add, reads=[Bp, Bxr, Bmod[l]], writes=[Bxo])
                fw.dma("sp", XB[:, oc, seq0 + s:seq0 + e], xo[:, 0:T], reads=[Bxo], writes=blkbufs(BXB, seq0 + s, T, oc))
                yield "O"

    def ffn_tile(l, kind, segs):
        r = kind
        T = sum(sg[1] for sg in segs)
        nseg = len(segs)
        offs = []
        o = 0
        for (x0, n, hl, hr) in segs:
            offs.append(o)
            o += n + 2
        W = o
        first = True
        subs = []
        for (x0, n, hl, hr), uo in zip(segs, offs):
            lo = x0 - 1 if hl else x0
            hi = x0 + n + 1 if hr else x0 + n
            if not hl:
                fw.op("dve", "memset", ap=h2v[:, :, uo:uo + 1], constant=0.0, writes=Bh2)
            if not hr:
                fw.op("dve", "memset", ap=h2v[:, :, uo + n + 1:uo + n + 2], constant=0.0, writes=Bh2)
            first = False
            t = lo
            while t < hi:
                m = min(NS, hi - t)
                cbase = uo + 1 + (t - x0)
                subs.append((t, m, (lambda c, cbase=cbase, m=m: h2v[:, c, cbase:cbase + m])))
                t += m
        for tk in norm_steps(XB, BXB, subs, l, r, 1, Bh2):
            yield tk or "N"
        yield "ENDNORM"
        nsl = (W + 511) // 512
        slices = []
        t = 0
        for i in range(nsl):
            m = (W - t + (nsl - i) - 1) // (nsl - i)
            slices.append((t, m))
            t += m
        for jg in range(NJ // 2):
            wa, Bwa = wload(w_cols(w_up_d, l, jg * 256, 256), 256)
            wb, Bwb = wload(w_cols(w_up_d, l, DFF + jg * 256, 256), 256)
            for jj in range(2):
                j = jg * 2 + jj
                res = []
                for half, (wt, Bw) in enumerate(((wa, Bwa), (wb, Bwb))):
                    outs = proj(wt, Bw, jj * 128, h2v, Bh2, slices)
                    U, BU = tmpf.get()
                    for (p, Bp, m), (c0, _) in zip(outs, slices):
                        fw.op("act", "copy", out=U[:, c0:c0 + m], in_=p[:, 0:m], reads=[Bp], writes=[BU])
                    acc, Bacc = tmpf.get()
                    ch = half * NJ + j
                    for (x0, n, hl, hr), uo in zip(segs, offs):
                        ao = uo - 2 * segs.index((x0, n, hl, hr))
                        fw.op("dve", "tensor_scalar", out=acc[:, ao:ao + n], in0=U[:, uo + 1:uo + 1 + n], scalar1=pvc(l, 128 + 88 + ch),
                              scalar2=pvc(l, 392 + ch), op0=ALU.mult, op1=ALU.add, reads=[BU, Bpv], writes=[Bacc])
                        fw.op("dve", "scalar_tensor_tensor", out=acc[:, ao:ao + n], in0=U[:, uo:uo + n], scalar=pvc(l, 128 + ch), in1=acc[:, ao:ao + n],
                              op0=ALU.mult, op1=ALU.add, reads=[BU, Bpv, Bacc], writes=[Bacc])
                        fw.op("dve", "scalar_tensor_tensor", out=acc[:, ao:ao + n], in0=U[:, uo + 2:uo + 2 + n], scalar=pvc(l, 128 + 176 + ch),
                              in1=acc[:, ao:ao + n], op0=ALU.mult, op1=ALU.add, reads=[BU, Bpv, Bacc], writes=[Bacc])
                    res.append((acc, Bacc))
                (aa, Baa), (ab, Bab) = res
                fw.op("act", "activation", out=aa[:, 0:T], in_=aa[:, 0:T], func=AF.Silu, reads=[Baa], writes=[Baa])
                fw.op("dve", "tensor_tensor", out=actv[:, j, 0:T], in0=aa[:, 0:T], in1=ab[:, 0:T], op=ALU.mult, reads=[Baa, Bab], writes=[Bact[j]])
        yield "PREOUT"
        for oc in range(KC):
            wdh = [wload(w_down_d[l].rearrange("(kc p) n -> p kc n", p=128)[:, hk * 22:(hk + 1) * 22, oc * 128:(oc + 1) * 128], 128, kc=22)
                   for hk in range(2)]
            xr, Bxr = xres.get()
            a0 = 0
            for (x0, n, hl, hr) in segs:
                fw.dma("sp", xr[:, a0:a0 + n], XB[:, oc, x0:x0 + n], reads=blkbufs(BXB, x0, n, oc), writes=[Bxr])
                a0 += n
            p, Bp = psAll.get()
            for k in range(NJ):
                wd, Bwd = wdh[k // 22]
                fw.op("pe", "matmul", out=p[:, 0:T], lhsT=wd[:, k % 22, 0:128], rhs=actv[:, k, 0:T], start=(k == 0), stop=(k == NJ - 1),
                      reads=[Bwd, Bact[k]], writes=[Bp], inc=(k == NJ - 1))
            xo, Bxo = xoutb.get()
            fw.op("dve", "scalar_tensor_tensor", out=xo[:, 0:T], in0=p[:, 0:T], scalar=mod[l][:, (80 + oc) * 2 + r:(80 + oc) * 2 + r + 1],
                  in1=xr[:, 0:T], op0=ALU.mult, op1=ALU.add, reads=[Bp, Bxr, Bmod[l]], writes=[Bxo])
            a0 = 0
            for (x0, n, hl, hr) in segs:
                fw.dma("sp", XA[:, oc, x0:x0 + n], xo[:, a0:a0 + n], reads=[Bxo], writes=blkbufs(BXA, x0, n, oc))
                a0 += n
            yield "O"

    def drive(tile_gens):
        prev = None
        for g in tile_gens:
            tk = next(g)
            assert tk == "P", tk
            while True:
                if prev is not None:
                    try:
                        next(prev)
                    except StopIteration:
                        prev = None
                if next(g) == "ENDNORM":
                    break
            if prev is not None:
                for _ in prev:
                    pass
            while next(g) != "PREOUT":
                pass
            prev = g
        if prev is not None:
            for _ in prev:
                pass

    for l in range(2):
        fw.barrier()
        layer_setup(l)
        fw.barrier()
        tiles = [mixer_tile(l, 0, s, TM, 0, LS, 0) for s in range(0, LS, TM)]
        tiles += [mixer_tile(l, 1, 0, LP, LS + pr * LP, LP, pr) for pr in range(2)]
        drive(tiles)
        fw.barrier()
        tiles = [ffn_tile(l, 0, [(s, TF, s > 0, s + TF < LS)]) for s in range(0, LS, TF)]
        tiles.append(ffn_tile(l, 1, [(LS, LP, False, False), (LS + LP, LP, False, False)]))
        drive(tiles)
    fw.barrier()

    nblk = NT // 128

    def fin_stage1(blk):
        return norm_A(XA, BXA, blk * 128, 128)

    def fin_stage2(blk, ctx):
        t0 = blk * 128
        st3, Bst, p_, Bp_, n_ = ctx
        norm_B(ctx, 0, 0, 2, (lambda c: st3[:, c, 0:128]), Bst)
        so, Bso = trst.get()
        for b4 in range(4):
            p, Bp = psAll.get()
            for q in range(4):
                c = b4 * 4 + q
                fw.op("pe", "transpose", out=p[:, q * 128:(q + 1) * 128], in_=st3[:, c, 0:128], identity=ident[:, :], reads=[Bst, Bident], writes=[Bp],
                      inc=(q == 3))
            if b4 % 2 == 0:
                fw.op("act", "copy", out=so[:, b4 * 512:(b4 + 1) * 512], in_=p[:, :], reads=[Bp], writes=[Bso])
            else:
                fw.op("dve", "tensor_copy", out=so[:, b4 * 512:(b4 + 1) * 512], in_=p[:, :], reads=[Bp], writes=[Bso])
        dst = ys_d[t0:t0 + 128, :] if t0 < LS else yp_d[t0 - LS:t0 - LS + 128, :]
        fw.dma("sp", dst, so[:, :], reads=[Bso], writes=[Buf()])

    ctx = fin_stage1(0)
    for blk in range(nblk):
        nxt = fin_stage1(blk + 1) if blk + 1 < nblk else None
        fin_stage2(blk, ctx)
        ctx = nxt

    fw.finish()
    fw.emit()
    return nc, fw


def _qperm():
    order = []
    for grp in range(2):
        for m in range(4):
            order += [grp * 8 + m, grp * 8 + 4 + m]
    cols = []
    for h in order:
        cols += list(range(h * 64, (h + 1) * 64))
    return np.array(cols + list(range(1024, 3584)), dtype=np.int64)


def _fm(v):
    v = np.asarray(v, dtype=np.float32)
    n = v.shape[-1] // 128
    v = v.reshape(v.shape[:-1] + (n, 128))
    return np.moveaxis(v, -1, 0)


def _consts(LS):
    ident = np.eye(128, dtype=np.float32)
    perm = np.zeros((128, 128), np.float32)
    for m in range(128):
        partner = m + 32 if (m % 64) < 32 else m - 32
        perm[partner, m] = 1.0
    rows = LS // 64
    row = np.repeat(np.arange(rows, dtype=np.float32), 64)
    colv = np.tile(np.arange(64, dtype=np.float32), rows)
    inv = (np.float32(10000.0) ** (-np.arange(16, dtype=np.float32) / np.float32(16))).astype(np.float32)
    ang = np.concatenate([row[:, None] * inv, colv[:, None] * inv], axis=-1).astype(np.float32)
    cos = np.cos(ang).astype(np.float32)
    sin = np.sin(ang).astype(np.float32)
    C = np.zeros((128, LS), np.float32)
    S = np.zeros((128, LS), np.float32)
    for p in range(128):
        d = p % 64
        C[p] = cos[:, d % 32]
        S[p] = -sin[:, d] if d < 32 else sin[:, d - 32]
    j = np.arange(128)[:, None]
    i = np.arange(128)[None, :]
    mP = np.tile((j >= i).astype(np.float32), (1, 4))
    mN = np.tile((j <= i).astype(np.float32), (1, 4))
    return dict(ident=ident, perm=perm, ropeC=C, ropeS=S, maskP=mP, maskN=mN)


def _pack(inp):
    pvec = np.zeros((128, NPV), np.float32)
    for l in range(2):
        b = l * PV_L
        pvec[:, b + 0:b + 16] = _fm(inp["g_norm1"][l])
        pvec[:, b + 16:b + 32] = _fm(inp["g_norm2"][l])
        pvec[:, b + 32:b + 128] = _fm(inp["b_ada"][l])
        pvec[:, b + 128:b + 392] = _fm(inp["ffn_dw_w"][l]).reshape(128, 3 * 88)
        pvec[:, b + 392:b + 480] = _fm(inp["ffn_dw_b"][l])
        pvec[:, b + 480:b + 604] = _fm(inp["conv_dw_w"][l]).reshape(128, 31 * 4)
        pvec[:, b + 604:b + 608] = _fm(inp["conv_dw_b"][l])
        pvec[:, b + 608:b + 612] = _fm(inp["conv_ln_g"][l])
        pvec[:, b + 612:b + 616] = _fm(inp["conv_ln_b"][l])
        pvec[:, b + 616:b + 620] = np.asarray(inp["gmlp_ln_g"][l], np.float32).T
        pvec[:, b + 620:b + 624] = np.asarray(inp["gmlp_ln_b"][l], np.float32).T
    pvec[:, 2 * PV_L:2 * PV_L + 16] = _fm(inp["g_final"])
    prow = np.zeros((2, NPR), np.float32)
    for l in range(2):
        prow[l, 0:512] = np.asarray(inp["gmlp_bs"][l], np.float32).reshape(-1)
        prow[l, 512:528] = np.asarray(inp["attn_sink"][l], np.float32).reshape(-1)
    return pvec, prow


_CACHE = {}


def kernel(**inp):
    inp = {k: np.asarray(v) for k, v in inp.items()}
    LS = inp["x_sample"].shape[1]
    n = 8
    if LS not in _CACHE:
        _CACHE[LS] = build(LS)[0]
    nc = _CACHE[LS]
    consts = _consts(LS)
    pvec, prow = _pack(inp)
    w_in = np.ascontiguousarray(inp["w_in"][:, :, _qperm()])
    shared = dict(pvec=pvec, prow=prow, w_ada=inp["w_ada"], w_in=w_in, w_out=inp["w_out"], w_up=inp["w_up"], w_down=inp["w_down"],
                  gmlp_ws=inp["gmlp_ws"], **consts)
    in_maps = []
    for c in range(n):
        cc = np.stack([inp["c"][c], inp["c_ctx"]], 0).astype(np.float32)
        ccT = np.ascontiguousarray(cc.reshape(2, 16, 128).transpose(2, 1, 0).reshape(128, 32))
        m = dict(shared)
        m.update(xs=np.ascontiguousarray(inp["x_sample"][c]),
                 xp=np.ascontiguousarray(inp["x_prompt"][2 * c:2 * c + 2].reshape(2 * LP, D)),
                 ck=np.ascontiguousarray(inp["cache_k"][c].reshape(2, 512, 256)),
                 cv=np.ascontiguousarray(inp["cache_v"][c].reshape(2, 512, 256)),
                 ccT=ccT)
        in_maps.append(m)
    res = run_bass_kernel_spmd(nc, in_maps, core_ids=list(range(n))).results
    y_sample = np.stack([r["ys"] for r in res], 0)
    y_prompt = np.concatenate([r["yp"].reshape(2, LP, D) for r in res], 0)
    nk = np.concatenate([r["nk"] for r in res], 0).reshape(16, 2, LP, 4, 64)
    nv = np.concatenate([r["nv"] for r in res], 0).reshape(16, 2, LP, 4, 64)
    return (y_prompt.astype(np.float32), y_sample.astype(np.float32), nk.astype(np.float32), nv.astype(np.float32))
```

```python
import numpy as np
import concourse.bass as bass
import concourse.mybir as mybir
from concourse.bass_utils import run_bass_kernel_spmd

F32 = mybir.dt.float32
BF16 = mybir.dt.bfloat16
AF = mybir.ActivationFunctionType
ALU = mybir.AluOpType
AX = mybir.AxisListType

SAME_ENGINE_SYNC = True
NORM_ODD_ENG = "dve"
D = 2048
KC = 16
DFF = 5632
NJ = 44
LP = 256
EPS = 1e-6
SCALE = 0.125
TM = 512
TF = 512
NS = 130
WSLOT = 4096
NW = 6
PV_L = 624
NPV = 2 * PV_L + 16
NPR = 528


class Buf:
    __slots__ = ("name", "w", "r")

    def __init__(self, name=""):
        self.name = name
        self.w = None
        self.r = []


class FW:
    def __init__(self, nc, n_dma_sems=40):
        self.nc = nc
        self.engs = ["pe", "act", "dve", "pool", "sp"]
        self.sem = {}
        self.cnt = {}
        for e in self.engs:
            self.sem[e] = nc.alloc_semaphore(name="S_" + e)
            self.cnt[e] = 0
        self.dsem = [nc.alloc_semaphore(name="D_%d" % i) for i in range(n_dma_sems)]
        self.dcnt = [0] * n_dma_sems
        self.dnext = {"sp": 0, "pool": n_dma_sems // 2, "act": 0}
        self.drange = {"sp": (0, n_dma_sems // 2), "pool": (n_dma_sems // 2, n_dma_sems), "act": (0, n_dma_sems // 2)}
        for i, s in enumerate(self.dsem):
            self.sem[("d", i)] = s
        self.known = {e: {} for e in self.engs}
        self.n_instr = 0
        self.n_wait = 0
        self.prog = {e: [] for e in self.engs}

    def _need(self, e, deps):
        best = {}
        for d in deps:
            if d is None:
                continue
            k, v = d
            if k == e and (not SAME_ENGINE_SYNC or e == "pe" or v > self.cnt[e]):
                continue
            if best.get(k, 0) < v:
                best[k] = v
        kn = self.known[e]
        for k, v in best.items():
            if kn.get(k, 0) >= v:
                continue
            self.prog[e].append((0, self.sem[k], v))
            self.n_wait += 1
            kn[k] = v

    def op(self, e, name, reads=(), writes=(), inc=True, **kw):
        deps = []
        for b in reads:
            deps.append(b.w)
        for b in writes:
            deps.append(b.w)
            deps.extend(b.r)
        self._need(e, deps)
        self.n_instr += 1
        if inc:
            self.cnt[e] += 1
            self.prog[e].append((1, (name, kw), self.sem[e], 1))
            tok = (e, self.cnt[e])
        else:
            self.prog[e].append((1, (name, kw), None, 0))
            tok = (e, self.cnt[e] + 1)
        for b in reads:
            b.r.append(tok)
        for b in writes:
            b.w = tok
            b.r = []

    def dma(self, q, out, in_, reads=(), writes=(), **kw):
        deps = []
        for b in reads:
            deps.append(b.w)
        for b in writes:
            deps.append(b.w)
            deps.extend(b.r)
        self._need(q, deps)
        i = self.dnext[q]
        lo_, hi_ = self.drange[q]
        self.dnext[q] = lo_ + (i + 1 - lo_) % (hi_ - lo_)
        if self.dcnt[i] > 0:
            self._need(q, [(("d", i), self.dcnt[i])])
        kw = dict(kw)
        kw["out"] = out
        kw["in_"] = in_
        self.prog[q].append((1, ("dma_start", kw), self.dsem[i], 16))
        self.dcnt[i] += 16
        self.n_instr += 1
        tok = (("d", i), self.dcnt[i])
        for b in reads:
            b.r.append(tok)
        for b in writes:
            b.w = tok
            b.r = []

    def barrier(self):
        deps = [(("d", i), self.dcnt[i]) for i in range(len(self.dsem)) if self.dcnt[i]]
        for e in self.engs:
            if self.cnt[e]:
                deps.append((e, self.cnt[e]))
        for e in self.engs:
            self._need(e, deps)

    def finish(self):
        deps = [(("d", i), self.dcnt[i]) for i in range(len(self.dsem)) if self.dcnt[i]]
        for e in self.engs:
            if e != "sp" and self.cnt[e]:
                deps.append((e, self.cnt[e]))
        self._need("sp", deps)

    def emit(self):
        nc = self.nc
        prog = self.prog

        def run(eng, lst):
            for it in lst:
                if it[0] == 0:
                    eng.wait_ge(it[1], it[2])
                else:
                    ins = getattr(eng, it[1][0])(**it[1][1])
                    if it[2] is not None:
                        ins.then_inc(it[2], it[3])

        with nc.Block() as block:
            @block.sync
            def _(eng):
                run(eng, prog["sp"])

            @block.tensor
            def _(eng):
                run(eng, prog["pe"])

            @block.scalar
            def _(eng):
                run(eng, prog["act"])

            @block.vector
            def _(eng):
                run(eng, prog["dve"])

            @block.gpsimd
            def _(eng):
                run(eng, prog["pool"])


def v3(ap2, c):
    return ap2.rearrange("p (c t) -> p c t", c=c)


class Rot:
    def __init__(self, items):
        self.items = items
        self.i = 0

    def get(self):
        it = self.items[self.i]
        self.i = (self.i + 1) % len(self.items)
        return it


def build(LS):
    NT = LS + 2 * LP
    nc = bass.Bass("TRN2", target_bir_lowering=False)
    fw = FW(nc)

    def din(name, shape):
        return nc.dram_tensor(name, list(shape), F32, kind="ExternalInput").ap()

    def dout(name, shape):
        return nc.dram_tensor(name, list(shape), F32, kind="ExternalOutput").ap()

    xs_d = din("xs", [LS, D])
    xp_d = din("xp", [2 * LP, D])
    ck_d = din("ck", [2, 512, 256])
    cv_d = din("cv", [2, 512, 256])
    ccT_d = din("ccT", [128, 32])
    pvec_d = din("pvec", [128, NPV])
    prow_d = din("prow", [2, NPR])
    w_ada_d = din("w_ada", [2, D, 6 * D])
    w_in_d = din("w_in", [2, D, 3584])
    w_out_d = din("w_out", [2, D, D])
    w_up_d = din("w_up", [2, D, 2 * DFF])
    w_down_d = din("w_down", [2, DFF, D])
    ws_d = din("gmlp_ws", [2, 4, 128, 128])
    ident_d = din("ident", [128, 128])
    perm_d = din("perm", [128, 128])
    ropeC_d = din("ropeC", [128, LS])
    ropeS_d = din("ropeS", [128, LS])
    maskP_d = din("maskP", [128, 512])
    maskN_d = din("maskN", [128, 512])
    ys_d = dout("ys", [LS, D])
    yp_d = dout("yp", [2 * LP, D])
    nk_d = dout("nk", [2, 2, LP, 256])
    nv_d = dout("nv", [2, 2, LP, 256])
    XA = nc.dram_tensor("XA", [KC, 128, NT], F32).ap().rearrange("c p t -> p c t")
    XB = nc.dram_tensor("XB", [KC, 128, NT], F32).ap().rearrange("c p t -> p c t")
    BXA = [[Buf() for _ in range(KC)] for _ in range(NT // 128 + 1)]
    BXB = [[Buf() for _ in range(KC)] for _ in range(NT // 128 + 1)]

    def blkbufs(B, t0, n, oc=None):
        out = []
        for bl in B[t0 // 128:(t0 + n - 1) // 128 + 1]:
            out += (bl if oc is None else [bl[oc]])
        return out

    def sb(name, shape, dt):
        return nc.alloc_sbuf_tensor("s_" + name, list(shape), dt)

    pv = sb("pv", [128, NPV], F32); Bpv = Buf()
    ident = sb("ident", [128, 128], F32); Bident = Buf()
    perm = sb("perm", [128, 128], F32); Bperm = Buf()
    ones_bf = sb("ones_bf", [128, 128], BF16); Bones = Buf()
    ones_f = sb("ones_f", [128, 128], F32)
    zo = sb("zo", [1, 128], BF16)
    epsT = sb("epsT", [128, 1], F32)
    maskP = sb("maskP", [128, 128], BF16); BmaskP = Buf()
    maskN = sb("maskN", [128, 128], BF16); BmaskN = Buf()
    ident_bf = sb("ident_bf", [128, 128], BF16)
    dgb = Rot([(sb("dg%d" % i, [128, 8 * 128], BF16), [Buf() for _ in range(8)]) for i in range(2)])
    scT = sb("scT", [128, 32], BF16); BscT = Buf()
    ccs = sb("ccs", [128, 32], F32); Bccs = Buf()
    mod = [sb("mod%d" % l, [128, 192], F32) for l in range(2)]; Bmod = [Buf(), Buf()]
    der = [sb("der%d" % l, [128, 2 * 32], F32) for l in range(2)]; Bder = [Buf(), Buf()]
    wsT = sb("wsT", [128, 4 * 128], BF16); BwsT = Buf()
    Bmat = sb("Bmat", [128, 4 * 128], F32); BBmat = Buf()
    sinkhl = sb("sinkhl", [1, 64], BF16); Bsink = Buf()
    sinkf = sb("sinkf", [1, 64], F32)
    KcT = sb("KcT", [128, 2 * 512], BF16); BKcT = Buf()
    Vc = sb("Vc", [128, 4 * 4 * 128], BF16); BVc = Buf()
    wring = Rot([(sb("wr%d" % i, [128, WSLOT], BF16), Buf("wr%d" % i)) for i in range(NW)])
    xst = Rot([(sb("xst%d" % i, [128, KC * NS], F32), Buf()) for i in range(3)])
    sqb = sb("sqb", [128, KC * NS], BF16); Bsqb = Buf()
    rsb = Rot([(sb("rsb%d" % i, [128, NS], F32), Buf()) for i in range(2)])
    BIG = 82 * 1024
    big = sb("big", [128, BIG // 2], BF16)
    tmpf_items = [(sb("tmpf%d" % i, [128, 520], F32), Buf()) for i in range(7)]
    tmpf = Rot(tmpf_items)
    tmpC = Rot(tmpf_items[0:2])
    tmpG = Rot(tmpf_items[4:7])
    xres = Rot(tmpf_items[0:2])
    xoutb = Rot(tmpf_items[2:4])
    pT = Rot([(sb("pT%d" % i, [128, 512], BF16), Buf()) for i in range(7)])
    smallf = sb("smallf", [128, 64], F32); Bsmall = Buf()
    ps = [nc.alloc_psum_tensor("ps%d" % i, [128, 512], F32) for i in range(8)]
    Bps = [Buf("ps%d" % i) for i in range(8)]
    psAll = Rot([(ps[i], Bps[i]) for i in range(8)])
    psS = Rot([(ps[i], Bps[i]) for i in range(3)])
    psC = Rot([(ps[i], Bps[i]) for i in range(4)])
    psO = Rot([(ps[i], Bps[i]) for i in (4, 5)])
    rcb = Rot([(sb("rcb%d" % i, [64, 512], F32), Buf()) for i in range(1)])

    def bigv(off_bytes, nbytes, dt):
        if dt == BF16:
            return big[:, off_bytes // 2:(off_bytes + nbytes) // 2]
        return big[:, off_bytes // 2:(off_bytes + nbytes) // 2].bitcast(F32)

    trst = Rot([(bigv(i * 8192, 8192, F32), Buf()) for i in range(2)])
    ckst = bigv(0, 4096, F32); Bckst = Buf()
    wsTf = bigv(4096, 2048, F32)
    o = 0
    h1 = bigv(o, KC * 768 * 2, BF16); o += KC * 768 * 2
    kT = bigv(o, 2 * 768 * 2, BF16); o += 2 * 768 * 2
    Vx = bigv(o, 6 * 4 * 128 * 2, BF16); o += 6 * 4 * 128 * 2
    qT = bigv(o, 8 * 512 * 2, BF16); o += 8 * 512 * 2
    cacc = bigv(o, 4 * 512 * 4, F32); o += 4 * 512 * 4
    mixT = bigv(o, KC * 512 * 2, BF16); o += KC * 512 * 2
    gu = bigv(o, 4 * 512 * 2, BF16); o += 4 * 512 * 2
    nTok = bigv(o, 4 * 512 * 2, BF16); o += 4 * 512 * 2
    glub = Rot([(bigv(o + i * 544 * 2, 544 * 2, BF16), Buf()) for i in range(2)]); o += 2 * 544 * 2
    ropC = bigv(o, 768 * 4, F32); o += 768 * 4
    ropS = bigv(o, 768 * 4, F32); o += 768 * 4
    assert o <= BIG, o
    BkT, BVx, BqT, Bgu, Brop = [Buf() for _ in range(5)]
    BnTok = [Buf() for _ in range(4)]
    Bh1 = [Buf() for _ in range(KC)]
    BmixT = [Buf() for _ in range(KC)]
    Bcacc = [Buf() for _ in range(4)]
    o = 0
    h2 = bigv(o, KC * (TF + 4) * 2, BF16); o += KC * (TF + 4) * 2
    actT = bigv(o, NJ * TF * 2, BF16); o += NJ * TF * 2
    assert o <= BIG, o
    Bh2 = [Buf() for _ in range(KC)]
    Bact = [Buf() for _ in range(NJ)]

    h1v = v3(h1, KC)
    kTv = v3(kT, 2)
    Vxv = Vx.rearrange("p (b k d) -> p b k d", b=6, k=4)
    qTv = v3(qT, 8)
    mixTv = v3(mixT, KC)
    guv = v3(gu, 4)
    nTokv = nTok.rearrange("p (b g c) -> p b g c", b=4, g=4)
    caccv = v3(cacc, 4)
    h2v = v3(h2, KC)
    actv = v3(actT, NJ)
    Vcv = Vc[:, :].rearrange("p (b k d) -> p b k d", b=4, k=4)
    KcTv = v3(KcT[:, :], 2)

    def pvc(l, off, n=1):
        b = l * PV_L + off
        return pv[:, b:b + n]

    def wload(src, ncols, kc=KC):
        t, B = wring.get()
        fw.dma("pool", v3(t[:, 0:kc * ncols], kc), src, writes=[B])
        return v3(t[:, 0:kc * ncols], kc), B

    def w_cols(wd, l, c0, w):
        return wd[l].rearrange("(kc p) n -> p kc n", p=128)[:, :, c0:c0 + w]

    def norm_L(X, BX, tok0, n):
        st, Bst = xst.get()
        st3 = v3(st[:, :], KC)
        fw.dma("sp", st3[:, :, 0:n], X[:, :, tok0:tok0 + n], reads=blkbufs(BX, tok0, n), writes=[Bst])
        return (st3, Bst, n)

    def norm_A(X, BX, tok0, n, pre=None):
        st3, Bst, n = pre if pre is not None else norm_L(X, BX, tok0, n)
        sq3 = v3(sqb[:, :], KC)
        fw.op("act", "activation", out=sq3[:, :, 0:n], in_=st3[:, :, 0:n], func=AF.Square, reads=[Bst], writes=[Bsqb])
        p, Bp = psAll.get()
        for c in range(KC):
            fw.op("pe", "matmul", out=p[:, 0:n], lhsT=ones_bf[:, :], rhs=sq3[:, c, 0:n], start=(c == 0), stop=(c == KC - 1),
                  reads=[Bsqb, Bones], writes=[Bp], inc=(c == KC - 1))
        return (st3, Bst, p, Bp, n)

    def norm_B(ctx, l, r, which, outfn, Bout):
        st3, Bst, p, Bp, n = ctx
        rs, Brs = rsb.get()
        fw.op("act", "activation", out=rs[:, 0:n], in_=p[:, 0:n], func=AF.Sqrt, scale=1.0 / D, bias=epsT[:, 0:1], reads=[Bp], writes=[Brs])
        fw.op("dve", "reciprocal", out=rs[:, 0:n], in_=rs[:, 0:n], reads=[Brs], writes=[Brs])
        fw.op("dve", "tensor_tensor", out=st3[:, :, 0:n], in0=st3[:, :, 0:n], in1=rs[:, 0:n].unsqueeze(1).broadcast_to([128, KC, n]),
              op=ALU.mult, reads=[Bst, Brs], writes=[Bst])
        for c in range(KC):
            if which == 2:
                fw.op("dve", "tensor_scalar", out=outfn(c), in0=st3[:, c, 0:n], scalar1=pv[:, 2 * PV_L + c:2 * PV_L + c + 1], scalar2=None,
                      op0=ALU.mult, reads=[Bst, Bpv], writes=[Bout[c] if isinstance(Bout, list) else Bout])
                continue
            A = der[l][:, r * 32 + which * 16 + c:r * 32 + which * 16 + c + 1]
            Bsh = mod[l][:, ((0 if which == 0 else 48) + c) * 2 + r:((0 if which == 0 else 48) + c) * 2 + r + 1]
            if c % 2 == 0:
                fw.op("act", "activation", out=outfn(c), in_=st3[:, c, 0:n], func=AF.Identity, scale=A, bias=Bsh,
                      reads=[Bst, Bder[l], Bmod[l]], writes=[Bout[c] if isinstance(Bout, list) else Bout])
            else:
                fw.op(NORM_ODD_ENG, "tensor_scalar", out=outfn(c), in0=st3[:, c, 0:n], scalar1=A, scalar2=Bsh, op0=ALU.mult, op1=ALU.add,
                      reads=[Bst, Bder[l], Bmod[l]], writes=[Bout[c] if isinstance(Bout, list) else Bout])

    def rmsnorm(X, BX, tok0, n, l, r, which, outfn, Bout):
        norm_B(norm_A(X, BX, tok0, n), l, r, which, outfn, Bout)

    def norm_steps(X, BX, subs, l, r, which, Bout):
        npre = min(3, len(subs))
        pre = [norm_L(X, BX, subs[j][0], subs[j][1]) for j in range(npre)]
        yield "P"
        ctx = norm_A(X, BX, subs[0][0], subs[0][1], pre[0])
        yield
        for j in range(len(subs)):
            nxt = None
            if j + 1 < len(subs):
                if j + 3 < len(subs) + 1 and j + 2 < len(subs) and j + 2 >= npre:
                    pre.append(norm_L(X, BX, subs[j + 2][0], subs[j + 2][1]))
                nxt = norm_A(X, BX, subs[j + 1][0], subs[j + 1][1], pre[j + 1] if j + 1 < len(pre) else None)
                yield
            norm_B(ctx, l, r, which, subs[j][2], Bout)
            yield
            ctx = nxt

    def proj(wt, Bw, wcol0, act3, Bact_, slices, kc=KC, rot=None, extra_reads=()):
        outs = []
        for (c0, n) in slices:
            p, Bp = (rot or psAll).get()
            outs.append((p, Bp, n))
        for k in range(kc):
            for si, (c0, n) in enumerate(slices):
                p, Bp, _ = outs[si]
                last = (k == kc - 1)
                fw.op("pe", "matmul", out=p[:, 0:n], lhsT=wt[:, k, wcol0:wcol0 + 128], rhs=act3[:, k, c0:c0 + n],
                      start=(k == 0), stop=last, reads=[Bw, (Bact_[k] if isinstance(Bact_, list) else Bact_)] + list(extra_reads), writes=[Bp], inc=last)
        return outs

    fw.dma("sp", pv[:, :], pvec_d, writes=[Bpv])
    fw.dma("sp", ident[:, :], ident_d, writes=[Bident])
    fw.dma("sp", perm[:, :], perm_d, writes=[Bperm])
    fw.dma("sp", ccs[:, :], ccT_d, writes=[Bccs])
    fw.dma("pool", maskP[:, :], maskP_d[:, 0:128], writes=[BmaskP])
    fw.dma("pool", maskN[:, :], maskN_d[:, 0:128], writes=[BmaskN])
    fw.op("dve", "tensor_copy", out=ident_bf[:, :], in_=ident[:, :], reads=[Bident], writes=[Bident])
    fw.op("dve", "memset", ap=ones_bf[:, :], constant=1.0, writes=[Bones])
    fw.op("dve", "memset", ap=ones_f[:, :], constant=1.0, writes=[Bones])
    fw.op("dve", "memset", ap=zo[:, 0:64], constant=0.0, writes=[Bones])
    fw.op("dve", "memset", ap=zo[:, 64:128], constant=1.0, writes=[Bones])
    fw.op("dve", "memset", ap=epsT[:, :], constant=EPS, writes=[Bones])
    fw.op("dve", "memset", ap=Vc[:, :], constant=1.0, writes=[BVc])
    fw.op("act", "activation", out=scT[:, :], in_=ccs[:, :], func=AF.Silu, reads=[Bccs], writes=[BscT])

    def ada_gen():
        for l in range(2):
            pm, Bpm = ps[7 - l], Bps[7 - l]
            for cg in range(48):
                wt, Bw = wload(w_cols(w_ada_d, l, cg * 256, 256), 256)
                for oc in range(2):
                    j = cg * 2 + oc
                    for kc in range(KC):
                        fw.op("pe", "matmul", out=pm[:, 2 * j:2 * j + 2], lhsT=wt[:, kc, oc * 128:(oc + 1) * 128], rhs=scT[:, 2 * kc:2 * kc + 2],
                              start=(kc == 0), stop=(kc == KC - 1), reads=[Bw, BscT], writes=[Bpm], inc=(kc == KC - 1))
                yield
            fw.op("dve", "tensor_tensor", out=mod[l][:, :].rearrange("p (j r) -> p j r", r=2), in0=pm[:, 0:192].rearrange("p (j r) -> p j r", r=2),
                  in1=pvc(l, 32, 96).unsqueeze(2).broadcast_to([128, 96, 2]), op=ALU.add, reads=[Bpm, Bpv], writes=[Bmod[l]])
            m3 = mod[l][:, :].rearrange("p (j r) -> p j r", r=2)
            for r in range(2):
                for which in range(2):
                    sc = m3[:, (16 if which == 0 else 64):(32 if which == 0 else 80), r]
                    dst = der[l][:, r * 32 + which * 16:r * 32 + which * 16 + 16]
                    fw.op("dve", "tensor_scalar", out=dst, in0=sc, scalar1=1.0, scalar2=None, op0=ALU.add, reads=[Bmod[l]], writes=[Bder[l]])
                    fw.op("dve", "tensor_tensor", out=dst, in0=dst, in1=pvc(l, 0 if which == 0 else 16, 16), op=ALU.mult,
                          reads=[Bder[l], Bpv], writes=[Bder[l]])
            yield

    psI = Rot([(ps[i], Bps[i]) for i in range(6)])

    def init_gen():
        for blk in range(NT // 128):
            t0 = blk * 128
            src = xs_d[t0:t0 + 128, :] if t0 < LS else xp_d[t0 - LS:t0 - LS + 128, :]
            st, Bst = trst.get()
            fw.dma("sp", st[:, :], src, writes=[Bst])
            so, Bso = xst.get()
            so3 = v3(so[:, 0:KC * 128], KC)
            for b4 in range(4):
                p, Bp = psI.get()
                for q in range(4):
                    c = b4 * 4 + q
                    fw.op("pe", "transpose", out=p[:, q * 128:(q + 1) * 128], in_=st[:, c * 128:(c + 1) * 128], identity=ident[:, :],
                          reads=[Bst, Bident], writes=[Bp], inc=(q == 3))
                if b4 % 2 == 0:
                    fw.op("act", "copy", out=so[:, b4 * 512:(b4 + 1) * 512], in_=p[:, :], reads=[Bp], writes=[Bso])
                else:
                    fw.op("dve", "tensor_copy", out=so[:, b4 * 512:(b4 + 1) * 512], in_=p[:, :], reads=[Bp], writes=[Bso])
            fw.dma("sp", XA[:, :, t0:t0 + 128], so3, reads=[Bso], writes=blkbufs(BXA, t0, 128))
            yield

    ga_, gi_ = ada_gen(), init_gen()
    a_alive = i_alive = True
    while a_alive or i_alive:
        for _ in range(3):
            if a_alive:
                try:
                    next(ga_)
                except StopIteration:
                    a_alive = False
        if i_alive:
            try:
                next(gi_)
            except StopIteration:
                i_alive = False

    def layer_setup(l):
        prow, Bprow = bigv(8192, 4096, F32), Buf()
        fw.dma("sp", prow[0:1, 0:NPR], prow_d[l:l + 1, :], writes=[Bprow])
        for g in range(4):
            st, Bst = tmpf.get()
            fw.dma("sp", st[:, 0:128], ws_d[l, g], writes=[Bst])
            p, Bp = psAll.get()
            fw.op("pe", "transpose", out=p[:, 0:128], in_=st[:, 0:128], identity=ident[:, :], reads=[Bst, Bident], writes=[Bp])
            fw.op("act", "copy", out=wsT[:, g * 128:(g + 1) * 128], in_=p[:, 0:128], reads=[Bp], writes=[BwsT])
            fw.op("dve", "tensor_copy", out=wsTf[:, g * 128:(g + 1) * 128], in_=p[:, 0:128], reads=[Bp], writes=[BwsT])
            p2, Bp2 = psAll.get()
            fw.op("pe", "matmul", out=p2[:, 0:128], lhsT=ones_f[:, :], rhs=wsTf[:, g * 128:(g + 1) * 128], start=True, stop=True,
                  reads=[BwsT, Bones], writes=[Bp2])
            p3, Bp3 = psAll.get()
            fw.op("pe", "matmul", out=p3[:, 0:128], lhsT=ones_f[0:1, :], rhs=prow[0:1, g * 128:(g + 1) * 128],
                  start=True, stop=True, reads=[Bprow, Bones], writes=[Bp3])
            t2, Bt2 = tmpf.get()
            fw.op("act", "copy", out=t2[:, 0:128], in_=p3[:, 0:128], reads=[Bp3], writes=[Bt2])
            fw.op("dve", "scalar_tensor_tensor", out=Bmat[:, g * 128:(g + 1) * 128], in0=p2[:, 0:128], scalar=pvc(l, 620 + g), in1=t2[:, 0:128],
                  op0=ALU.mult, op1=ALU.add, reads=[Bp2, Bt2, Bpv], writes=[BBmat])
        fw.op("act", "activation", out=sinkf[:, 0:16], in_=prow[0:1, 512:528], func=AF.Exp, reads=[Bprow], writes=[Bsink])
        fw.op("dve", "tensor_copy", out=sinkhl[:, 0:16], in_=sinkf[:, 0:16], reads=[Bsink], writes=[Bsink])
        fw.op("dve", "tensor_copy", out=sinkf[:, 16:32], in_=sinkhl[:, 0:16], reads=[Bsink], writes=[Bsink])
        fw.op("dve", "tensor_tensor", out=sinkf[:, 32:48], in0=sinkf[:, 0:16], in1=sinkf[:, 16:32], op=ALU.subtract, reads=[Bsink], writes=[Bsink])
        fw.op("dve", "tensor_copy", out=sinkhl[:, 16:32], in_=sinkf[:, 32:48], reads=[Bsink], writes=[Bsink])
        fw.dma("sp", ckst[:, :].rearrange("p (b f) -> p b f", b=4), ck_d[l].rearrange("(b p) f -> p b f", p=128), writes=[Bckst])
        for b in range(4):
            for ch in range(2):
                p, Bp = psAll.get()
                fw.op("pe", "transpose", out=p[:, 0:128], in_=ckst[:, b * 256 + ch * 128:b * 256 + (ch + 1) * 128], identity=ident[:, :],
                      reads=[Bckst, Bident], writes=[Bp])
                fw.op("act", "copy", out=KcTv[:, ch, b * 128:(b + 1) * 128], in_=p[:, 0:128], reads=[Bp], writes=[BKcT])
        for b in range(4):
            fw.dma("pool", Vcv[:, b, :, 0:64], cv_d[l, b * 128:(b + 1) * 128, :].rearrange("p (k d) -> p k d", k=4), writes=[BVc])

    def mixer_tile(l, kind, s, T, seq0, L, pr):
        r = kind
        sample = (kind == 0)
        e = s + T
        has_l = s > 0
        has_r = e < L
        lo = s - 128 if has_l else s
        hi = e + 128 if has_r else e
        col = lambda t: t - (s - 128)
        nb = T // 128
        subs = []
        t = lo
        while t < hi:
            n = min(128, hi - t)
            subs.append((seq0 + t, n, (lambda c, t=t, n=n: h1v[:, c, col(t):col(t) + n])))
            t += n
        for tk in norm_steps(XA, BXA, subs, l, r, 0, Bh1):
            yield tk or "N"
        yield "ENDNORM"
        if sample:
            fw.dma("sp", ropC[:, col(lo):col(hi)], ropeC_d[:, lo:hi], writes=[Brop])
            fw.dma("sp", ropS[:, col(lo):col(hi)], ropeS_d[:, lo:hi], writes=[Brop])

        def rope_evac(p, Bp, n, c0, dst, Bdst):
            kf, Bkf = tmpf.get()
            fw.op("act", "copy", out=kf[:, 0:n], in_=p[:, 0:n], reads=[Bp], writes=[Bkf])
            p2, Bp2 = psAll.get()
            fw.op("pe", "matmul", out=p2[:, 0:n], lhsT=perm[:, :], rhs=kf[:, 0:n], start=True, stop=True, reads=[Bkf, Bperm], writes=[Bp2])
            t1, Bt1 = tmpf.get()
            fw.op("dve", "tensor_tensor", out=t1[:, 0:n], in0=kf[:, 0:n], in1=ropC[:, c0:c0 + n], op=ALU.mult, reads=[Bkf, Brop], writes=[Bt1])
            t2, Bt2 = tmpf.get()
            fw.op("dve", "tensor_tensor", out=t2[:, 0:n], in0=p2[:, 0:n], in1=ropS[:, c0:c0 + n], op=ALU.mult, reads=[Bp2, Brop], writes=[Bt2])
            fw.op("dve", "tensor_tensor", out=dst, in0=t1[:, 0:n], in1=t2[:, 0:n], op=ALU.add, reads=[Bt1, Bt2], writes=[Bdst])

        wk, Bwk = wload(w_cols(w_in_d, l, 1024, 256), 256)
        wv, Bwv = wload(w_cols(w_in_d, l, 1280, 256), 256)
        nall = hi - lo
        kslices = []
        t = lo
        while t < hi:
            n = min(384, hi - t)
            kslices.append((col(t), n))
            t += n
        for ch in range(2):
            outs = proj(wk, Bwk, ch * 128, h1v, Bh1, kslices)
            for (p, Bp, n), (c0, _) in zip(outs, kslices):
                if sample:
                    rope_evac(p, Bp, n, c0, kTv[:, ch, c0:c0 + n], BkT)
                else:
                    fw.op("act", "copy", out=kTv[:, ch, c0:c0 + n], in_=p[:, 0:n], reads=[Bp], writes=[BkT])
        for b in range((hi - lo) // 128):
            c0 = col(lo) + b * 128
            bi = c0 // 128
            p, Bp = psAll.get()
            for k in range(KC):
                fw.op("pe", "matmul", out=p[:, 0:256], lhsT=h1v[:, k, c0:c0 + 128], rhs=wv[:, k, 0:256], start=(k == 0), stop=(k == KC - 1),
                      reads=[Bh1[k], Bwv], writes=[Bp], inc=(k == KC - 1))
            fw.op("act", "copy", out=Vxv[:, bi, :, 0:64], in_=p[:, 0:256].rearrange("p (k d) -> p k d", k=4), reads=[Bp], writes=[BVx])
            fw.op("dve", "memset", ap=Vxv[:, bi, :, 64:128], constant=1.0, writes=[BVx])
            if not sample:
                to, Bto = xoutb.get()
                fw.op("dve", "tensor_copy", out=to[:, 0:256], in_=p[:, 0:256], reads=[Bp], writes=[Bto])
                fw.dma("sp", nv_d[pr, l, s + b * 128:s + (b + 1) * 128, :], to[:, 0:256], reads=[Bto], writes=[Buf()])
                p2, Bp2 = psAll.get()
                for k in range(KC):
                    fw.op("pe", "matmul", out=p2[:, 0:256], lhsT=h1v[:, k, c0:c0 + 128], rhs=wk[:, k, 0:256], start=(k == 0), stop=(k == KC - 1),
                          reads=[Bh1[k], Bwk], writes=[Bp2], inc=(k == KC - 1))
                to2, Bto2 = xoutb.get()
                fw.op("act", "copy", out=to2[:, 0:256], in_=p2[:, 0:256], reads=[Bp2], writes=[Bto2])
                fw.dma("sp", nk_d[pr, l, s + b * 128:s + (b + 1) * 128, :], to2[:, 0:256], reads=[Bto2], writes=[Buf()])
        cs = col(s)
        for half in range(4):
            wq, Bwq = wload(w_cols(w_in_d, l, half * 256, 256), 256)
            for cc in range(2):
                ch = half * 2 + cc
                (p, Bp, n), = proj(wq, Bwq, cc * 128, h1v, Bh1, [(cs, T)])
                if sample:
                    rope_evac(p, Bp, T, cs, qTv[:, ch, 0:T], BqT)
                else:
                    fw.op("act", "copy", out=qTv[:, ch, 0:T], in_=p[:, 0:T], reads=[Bp], writes=[BqT])
        def attention_gen():
            groups = [(qb, kv) for qb in range(nb) for kv in range(4)]

            def stage1(qb, kv):
                plo = (kv % 2) * 64
                kch = kv // 2
                qc0 = (kv // 2) * 4
                keys = []
                if sample:
                    gb = (s // 128) + qb
                    for dlt in (-1, 0, 1):
                        g2 = gb + dlt
                        if g2 < 0 or g2 >= L // 128:
                            continue
                        c0 = col(g2 * 128)
                        keys.append((kTv[plo:plo + 64, kch, c0:c0 + 128], Vxv[:, c0 // 128, kv, :], dlt, [BkT], [BVx]))
                    for b in range(4):
                        keys.append((KcTv[plo:plo + 64, kch, b * 128:(b + 1) * 128], Vcv[:, b, kv, :], 0, [BKcT], [BVc]))
                else:
                    for b in range(L // 128):
                        c0 = col(b * 128)
                        keys.append((kTv[plo:plo + 64, kch, c0:c0 + 128], Vxv[:, c0 // 128, kv, :], 0, [BkT], [BVx]))
                rhs_q = qTv[plo:plo + 64, qc0:qc0 + 4, qb * 128:(qb + 1) * 128]
                pts = []
                for (kap, vap, dlt, kr, vr) in keys:
                    p, Bp = psS.get()
                    fw.op("pe", "matmul", out=p[:, :], lhsT=kap, rhs=rhs_q, start=True, stop=True, reads=kr + [BqT], writes=[Bp])
                    pt, Bpt = pT.get()
                    fw.op("act", "activation", out=pt[:, :], in_=p[:, :], func=AF.Exp, scale=SCALE, reads=[Bp], writes=[Bpt])
                    if dlt != 0:
                        mk, Bmk = (maskP, BmaskP) if dlt == -1 else (maskN, BmaskN)
                        fw.op("dve", "tensor_tensor", out=pt[:, :].rearrange("p (g t) -> p g t", g=4), in0=pt[:, :].rearrange("p (g t) -> p g t", g=4),
                              in1=mk[:, :].unsqueeze(1).broadcast_to([128, 4, 128]), op=ALU.mult, reads=[Bpt, Bmk], writes=[Bpt])
                    pts.append((pt, Bpt, vap, vr))
                return pts

            def stage2(qb, kv, pts):
                po, Bpo = psO.get()
                for i, (pt, Bpt, vap, vr) in enumerate(pts):
                    fw.op("pe", "matmul", out=po[:, :], lhsT=vap, rhs=pt[:, :], start=(i == 0), stop=False, reads=vr + [Bpt], writes=[Bpo], inc=False)
                for hl in range(2):
                    fw.op("pe", "matmul", out=po[:, :], lhsT=zo[0:1, :],
                          rhs=sinkhl[0:1, hl * 16 + kv * 4:hl * 16 + kv * 4 + 4].unsqueeze(2).broadcast_to([1, 4, 128]),
                          start=False, stop=(hl == 1), reads=[Bsink, Bones], writes=[Bpo], inc=(hl == 1))
                rc, Brc = rcb.get()
                fw.op("dve", "reciprocal", out=rc[0:64, 0:512], in_=po[64:128, :], reads=[Bpo], writes=[Brc])
                po4 = po[0:64, :].rearrange("p (a b t) -> p a b t", a=2, b=2)
                rc4 = rc[0:64, 0:512].rearrange("p (a b t) -> p a b t", a=2, b=2)
                for par in range(2):
                    fw.op("dve", "tensor_tensor", out=mixTv[par * 64:par * 64 + 64, kv * 2:kv * 2 + 2, qb * 128:(qb + 1) * 128],
                          in0=po4[:, :, par, :], in1=rc4[:, :, par, :], op=ALU.mult, reads=[Bpo, Brc], writes=BmixT[kv * 2:kv * 2 + 2])

            for (qb, kv) in groups:
                cur = stage1(qb, kv)
                yield
                stage2(qb, kv, cur)
                yield

        def conv_gen():
            clo = max(s - 15, 0) if has_l else s
            chi = min(e + 15, L) if has_r else e
            gcol = lambda t: t - (s - 15)
            cslices = []
            t = clo
            ncs = (chi - clo + 511) // 512
            for i in range(ncs):
                n = (chi - t + (ncs - i) - 1) // (ncs - i)
                cslices.append((col(t), n, gcol(t)))
                t += n
            wa2 = [wload(w_cols(w_in_d, l, 1536 + i * 256, 256), 256) for i in range(2)]
            wg2 = [wload(w_cols(w_in_d, l, 2048 + i * 256, 256), 256) for i in range(2)]
            p1, Bp1 = ps[6], Bps[6]
            p2, Bp2 = ps[7], Bps[7]
            for c in range(4):
                gl, Bglu = glub.get()
                if not has_l:
                    fw.op("dve", "memset", ap=gl[:, 0:15], constant=0.0, writes=[Bglu])
                if not has_r:
                    fw.op("dve", "memset", ap=gl[:, 15 + T:30 + T], constant=0.0, writes=[Bglu])
                (wa, Bwa), (wg, Bwg) = wa2[c // 2], wg2[c // 2]
                oa = proj(wa, Bwa, (c % 2) * 128, h1v, Bh1, [(c0, n) for (c0, n, _) in cslices], rot=psC)
                og = proj(wg, Bwg, (c % 2) * 128, h1v, Bh1, [(c0, n) for (c0, n, _) in cslices], rot=psC)
                for (pa, Bpa, n), (pg, Bpg, _), (_, _, g0) in zip(oa, og, cslices):
                    sg, Bsg = tmpC.get()
                    fw.op("act", "activation", out=sg[:, 0:n], in_=pg[:, 0:n], func=AF.Sigmoid, reads=[Bpg], writes=[Bsg])
                    fw.op("dve", "tensor_tensor", out=gl[:, g0:g0 + n], in0=pa[:, 0:n], in1=sg[:, 0:n], op=ALU.mult, reads=[Bpa, Bsg], writes=[Bglu])
                yield
                pc, Bpc = ps[3], Bps[3]
                for k0 in range(0, 31, 8):
                    dg, Bdg = dgb.get()
                    kn = min(8, 31 - k0)
                    for j in range(kn):
                        if j % 2 == 0:
                            fw.op("act", "activation", out=dg[:, j * 128:(j + 1) * 128], in_=ident_bf[:, :], func=AF.Copy, scale=pvc(l, 480 + (k0 + j) * 4 + c),
                                  reads=[Bident, Bpv], writes=[Bdg[j]])
                        else:
                            fw.op("dve", "tensor_scalar", out=dg[:, j * 128:(j + 1) * 128], in0=ident_bf[:, :], scalar1=pvc(l, 480 + (k0 + j) * 4 + c), scalar2=None,
                                  op0=ALU.mult, reads=[Bident, Bpv], writes=[Bdg[j]])
                    for j in range(kn):
                        k = k0 + j
                        fw.op("pe", "matmul", out=pc[:, 0:T], lhsT=dg[:, j * 128:(j + 1) * 128], rhs=gl[:, k:k + T], start=(k == 0), stop=(k == 30),
                              reads=[Bdg[j], Bglu], writes=[Bpc], inc=(j == kn - 1))
                    yield
                fw.op("act", "activation", out=caccv[:, c, 0:T], in_=pc[:, 0:T], func=AF.Identity, scale=1.0, bias=pvc(l, 604 + c),
                      reads=[Bpc, Bpv], writes=[Bcacc[c]])
                sq, Bsq = tmpC.get()
                fw.op("act", "activation", out=sq[:, 0:T], in_=caccv[:, c, 0:T], func=AF.Square, reads=[Bcacc[c]], writes=[Bsq])
                fw.op("pe", "matmul", out=p1[:, 0:T], lhsT=ones_f[:, :], rhs=caccv[:, c, 0:T], start=(c == 0), stop=(c == 3), reads=[Bcacc[c], Bones],
                      writes=[Bp1])
                fw.op("pe", "matmul", out=p2[:, 0:T], lhsT=ones_f[:, :], rhs=sq[:, 0:T], start=(c == 0), stop=(c == 3), reads=[Bsq, Bones],
                      writes=[Bp2])
                yield
            mu, Bmu = tmpf_items[2]
            fw.op("act", "activation", out=mu[:, 0:T], in_=p1[:, 0:T], func=AF.Identity, scale=1.0 / 512, reads=[Bp1], writes=[Bmu])
            msq, Bmsq = tmpf_items[3]
            fw.op("dve", "tensor_tensor", out=msq[:, 0:T], in0=mu[:, 0:T], in1=mu[:, 0:T], op=ALU.mult, reads=[Bmu], writes=[Bmsq])
            fw.op("dve", "scalar_tensor_tensor", out=msq[:, 0:T], in0=p2[:, 0:T], scalar=1.0 / 512, in1=msq[:, 0:T], op0=ALU.mult, op1=ALU.subtract,
                  reads=[Bp2, Bmsq], writes=[Bmsq])
            fw.op("act", "activation", out=msq[:, 0:T], in_=msq[:, 0:T], func=AF.Sqrt, scale=1.0, bias=epsT[:, 0:1], reads=[Bmsq], writes=[Bmsq])
            fw.op("dve", "reciprocal", out=msq[:, 0:T], in_=msq[:, 0:T], reads=[Bmsq], writes=[Bmsq])
            yield
            for c in range(4):
                fw.op("dve", "tensor_tensor", out=caccv[:, c, 0:T], in0=caccv[:, c, 0:T], in1=mu[:, 0:T], op=ALU.subtract, reads=[Bcacc[c], Bmu], writes=[Bcacc[c]])
                fw.op("dve", "tensor_tensor", out=caccv[:, c, 0:T], in0=caccv[:, c, 0:T], in1=msq[:, 0:T], op=ALU.mult, reads=[Bcacc[c], Bmsq], writes=[Bcacc[c]])
                fw.op("act", "activation", out=mixTv[:, 8 + c, 0:T], in_=caccv[:, c, 0:T], func=AF.Silu, scale=pvc(l, 608 + c), bias=pvc(l, 612 + c),
                      reads=[Bcacc[c], Bpv], writes=[BmixT[8 + c]])
                yield

        def gmlp_gen():
            wgu2 = [wload(w_cols(w_in_d, l, 2560 + i * 256, 256), 256) for i in range(2)]
            wgv2 = [wload(w_cols(w_in_d, l, 3072 + i * 256, 256), 256) for i in range(2)]
            for c in range(4):
                wgu, Bwgu = wgu2[c // 2]
                (p, Bp, n), = proj(wgu, Bwgu, (c % 2) * 128, h1v, Bh1, [(cs, T)], rot=psS)
                fw.op("act", "activation", out=guv[:, c, 0:T], in_=p[:, 0:T], func=AF.Gelu_apprx_tanh, reads=[Bp], writes=[Bgu])
                yield
            for b in range(nb):
                c0 = cs + b * 128
                p, Bp = psS.get()
                for hv in range(2):
                    wgv, Bwgv = wgv2[hv]
                    for k in range(KC):
                        fw.op("pe", "matmul", out=p[:, hv * 256:(hv + 1) * 256], lhsT=h1v[:, k, c0:c0 + 128], rhs=wgv[:, k, :], start=(k == 0),
                              stop=(k == KC - 1), reads=[Bh1[k], Bwgv], writes=[Bp], inc=(k == KC - 1))
                gvt, Bgvt = tmpG.get()
                fw.op("act", "activation", out=gvt[:, 0:512], in_=p[:, :], func=AF.Gelu_apprx_tanh, reads=[Bp], writes=[Bgvt])
                g2, Bg2 = tmpG.get()
                fw.op("act", "activation", out=g2[:, 0:512], in_=gvt[:, 0:512], func=AF.Square, reads=[Bgvt], writes=[Bg2])
                yield
                sm = smallf
                fw.op("dve", "reduce_sum", out=sm[:, 0:4], in_=gvt[:, 0:512].rearrange("p (g c) -> p g c", g=4), axis=AX.X, reads=[Bgvt], writes=[Bsmall])
                fw.op("dve", "reduce_sum", out=sm[:, 4:8], in_=g2[:, 0:512].rearrange("p (g c) -> p g c", g=4), axis=AX.X, reads=[Bg2], writes=[Bsmall])
                fw.op("dve", "tensor_scalar", out=sm[:, 0:8], in0=sm[:, 0:8], scalar1=1.0 / 128, scalar2=None, op0=ALU.mult, reads=[Bsmall], writes=[Bsmall])
                fw.op("dve", "tensor_tensor", out=sm[:, 8:12], in0=sm[:, 0:4], in1=sm[:, 0:4], op=ALU.mult, reads=[Bsmall], writes=[Bsmall])
                fw.op("dve", "tensor_tensor", out=sm[:, 8:12], in0=sm[:, 4:8], in1=sm[:, 8:12], op=ALU.subtract, reads=[Bsmall], writes=[Bsmall])
                fw.op("act", "activation", out=sm[:, 8:12], in_=sm[:, 8:12], func=AF.Sqrt, scale=1.0, bias=epsT[:, 0:1], reads=[Bsmall], writes=[Bsmall])
                fw.op("dve", "reciprocal", out=sm[:, 8:12], in_=sm[:, 8:12], reads=[Bsmall], writes=[Bsmall])
                sp_, Bsp = psS.get()
                for g in range(4):
                    fw.op("dve", "tensor_scalar", out=nTokv[:, b, g, :], in0=gvt[:, g * 128:(g + 1) * 128], scalar1=sm[:, g:g + 1], scalar2=sm[:, 8 + g:9 + g],
                          op0=ALU.subtract, op1=ALU.mult, reads=[Bgvt, Bsmall], writes=[BnTok[g]])
                for g in range(4):
                    fw.op("pe", "matmul", out=sp_[:, g * 128:(g + 1) * 128], lhsT=nTokv[:, b, g, :], rhs=wsT[:, g * 128:(g + 1) * 128], start=True, stop=True,
                          reads=[BnTok[g], BwsT], writes=[Bsp])
                sv, Bsv = tmpG.get()
                for g in range(4):
                    fw.op("dve", "scalar_tensor_tensor", out=sv[:, g * 128:(g + 1) * 128], in0=sp_[:, g * 128:(g + 1) * 128], scalar=pvc(l, 616 + g),
                          in1=Bmat[:, g * 128:(g + 1) * 128], op0=ALU.mult, op1=ALU.add, reads=[Bsp, BBmat, Bpv], writes=[Bsv])
                fw.op("dve", "tensor_tensor", out=mixTv[:, 12:16, b * 128:(b + 1) * 128], in0=guv[:, :, b * 128:(b + 1) * 128],
                      in1=sv[:, 0:512].rearrange("p (g t) -> p g t", g=4), op=ALU.mult, reads=[Bgu, Bsv], writes=BmixT[12:16])
                yield

        def step(g_):
            try:
                next(g_)
                return True
            except StopIteration:
                return False

        ga, gg, gc = attention_gen(), gmlp_gen(), conv_gen()
        a_alive = True
        while step(gg):
            if a_alive:
                a_alive = step(ga)
        c_alive = True
        while a_alive or c_alive:
            if a_alive:
                a_alive = step(ga)
            if c_alive:
                c_alive = step(gc)
        yield "PREOUT"
        for og in range(8):
            wo, Bwo = wload(w_cols(w_out_d, l, og * 256, 256), 256)
            for cc in range(2):
                oc = og * 2 + cc
                xr, Bxr = xres.get()
                fw.dma("sp", xr[:, 0:T], XA[:, oc, seq0 + s:seq0 + e], reads=blkbufs(BXA, seq0 + s, T, oc), writes=[Bxr])
                (p, Bp, n), = proj(wo, Bwo, cc * 128, mixTv, BmixT, [(0, T)])
                xo, Bxo = xoutb.get()
                fw.op("dve", "scalar_tensor_tensor", out=xo[:, 0:T], in0=p[:, 0:T], scalar=mod[l][:, (32 + oc) * 2 + r:(32 + oc) * 2 + r + 1],
                      in1=xr[:, 0:T], op0=ALU.mult, op1=ALU.add, reads=[Bp, Bxr, Bmod[l]], writes=[Bxo])
                fw.dma("sp", XB[:, oc, seq0 + s:seq0 + e], xo[:, 0:T], reads=[Bxo], writes=blkbufs(BXB, seq0 + s, T, oc))
                yield "O"

    def ffn_tile(l, kind, segs):
        r = kind
        T = sum(sg[1] for sg in segs)
        nseg = len(segs)
        offs = []
        o = 0
        for (x0, n, hl, hr) in segs:
            offs.append(o)
            o += n + 2
        W = o
        first = True
        subs = []
        for (x0, n, hl, hr), uo in zip(segs, offs):
            lo = x0 - 1 if hl else x0
            hi = x0 + n + 1 if hr else x0 + n
            if not hl:
                fw.op("dve", "memset", ap=h2v[:, :, uo:uo + 1], constant=0.0, writes=Bh2)
            if not hr:
                fw.op("dve", "memset", ap=h2v[:, :, uo + n + 1:uo + n + 2], constant=0.0, writes=Bh2)
            first = False
            t = lo
            while t < hi:
                m = min(NS, hi - t)
                cbase = uo + 1 + (t - x0)
                subs.append((t, m, (lambda c, cbase=cbase, m=m: h2v[:, c, cbase:cbase + m])))
                t += m
        for tk in norm_steps(XB, BXB, subs, l, r, 1, Bh2):
            yield tk or "N"
        yield "ENDNORM"
        nsl = (W + 511) // 512
        slices = []
        t = 0
        for i in range(nsl):
            m = (W - t + (nsl - i) - 1) // (nsl - i)
            slices.append((t, m))
            t += m
        for jg in range(NJ // 2):
            wa, Bwa = wload(w_cols(w_up_d, l, jg * 256, 256), 256)
            wb, Bwb = wload(w_cols(w_up_d, l, DFF + jg * 256, 256), 256)
            for jj in range(2):
                j = jg * 2 + jj
                res = []
                for half, (wt, Bw) in enumerate(((wa, Bwa), (wb, Bwb))):
                    outs = proj(wt, Bw, jj * 128, h2v, Bh2, slices)
                    U, BU = tmpf.get()
                    for (p, Bp, m), (c0, _) in zip(outs, slices):
                        fw.op("act", "copy", out=U[:, c0:c0 + m], in_=p[:, 0:m], reads=[Bp], writes=[BU])
                    acc, Bacc = tmpf.get()
                    ch = half * NJ + j
                    for (x0, n, hl, hr), uo in zip(segs, offs):
                        ao = uo - 2 * segs.index((x0, n, hl, hr))
                        fw.op("dve", "tensor_scalar", out=acc[:, ao:ao + n], in0=U[:, uo + 1:uo + 1 + n], scalar1=pvc(l, 128 + 88 + ch),
                              scalar2=pvc(l, 392 + ch), op0=ALU.mult, op1=ALU.add, reads=[BU, Bpv], writes=[Bacc])
                        fw.op("dve", "scalar_tensor_tensor", out=acc[:, ao:ao + n], in0=U[:, uo:uo + n], scalar=pvc(l, 128 + ch), in1=acc[:, ao:ao + n],
                              op0=ALU.mult, op1=ALU.add, reads=[BU, Bpv, Bacc], writes=[Bacc])
                        fw.op("dve", "scalar_tensor_tensor", out=acc[:, ao:ao + n], in0=U[:, uo + 2:uo + 2 + n], scalar=pvc(l, 128 + 176 + ch),
                              in1=acc[:, ao:ao + n], op0=ALU.mult, op1=ALU.add, reads=[BU, Bpv, Bacc], writes=[Bacc])
                    res.append((acc, Bacc))
                (aa, Baa), (ab, Bab) = res
                fw.op("act", "activation", out=aa[:, 0:T], in_=aa[:, 0:T], func=AF.Silu, reads=[Baa], writes=[Baa])
                fw.op("dve", "tensor_tensor", out=actv[:, j, 0:T], in0=aa[:, 0:T], in1=ab[:, 0:T], op=ALU.mult, reads=[Baa, Bab], writes=[Bact[j]])
        yield "PREOUT"
        for oc in range(KC):
            wdh = [wload(w_down_d[l].rearrange("(kc p) n -> p kc n", p=128)[:, hk * 22:(hk + 1) * 22, oc * 128:(oc + 1) * 128], 128, kc=22)
                   for hk in range(2)]
            xr, Bxr = xres.get()
            a0 = 0
            for (x0, n, hl, hr) in segs:
                fw.dma("sp", xr[:, a0:a0 + n], XB[:, oc, x0:x0 + n], reads=blkbufs(BXB, x0, n, oc), writes=[Bxr])
                a0 += n
            p, Bp = psAll.get()
            for k in range(NJ):
                wd, Bwd = wdh[k // 22]
                fw.op("pe", "matmul", out=p[:, 0:T], lhsT=wd[:, k % 22, 0:128], rhs=actv[:, k, 0:T], start=(k == 0), stop=(k == NJ - 1),
                      reads=[Bwd, Bact[k]], writes=[Bp], inc=(k == NJ - 1))
            xo, Bxo = xoutb.get()
            fw.op("dve", "scalar_tensor_tensor", out=xo[:, 0:T], in0=p[:, 0:T], scalar=mod[l][:, (80 + oc) * 2 + r:(80 + oc) * 2 + r + 1],
                  in1=xr[:, 0:T], op0=ALU.mult, op1=ALU.add, reads=[Bp, Bxr, Bmod[l]], writes=[Bxo])
            a0 = 0
            for (x0, n, hl, hr) in segs:
                fw.dma("sp", XA[:, oc, x0:x0 + n], xo[:, a0:a0 + n], reads=[Bxo], writes=blkbufs(BXA, x0, n, oc))
                a0 += n
            yield "O"

    def drive(tile_gens):
        prev = None
        for g in tile_gens:
            tk = next(g)
            assert tk == "P", tk
            while True:
                if prev is not None:
                    try:
                        next(prev)
                    except StopIteration:
                        prev = None
                if next(g) == "ENDNORM":
                    break
            if prev is not None:
                for _ in prev:
                    pass
            while next(g) != "PREOUT":
                pass
            prev = g
        if prev is not None:
            for _ in prev:
                pass

    for l in range(2):
        fw.barrier()
        layer_setup(l)
        fw.barrier()
        tiles = [mixer_tile(l, 0, s, TM, 0, LS, 0) for s in range(0, LS, TM)]
        tiles += [mixer_tile(l, 1, 0, LP, LS + pr * LP, LP, pr) for pr in range(2)]
        drive(tiles)
        fw.barrier()
        tiles = [ffn_tile(l, 0, [(s, TF, s > 0, s + TF < LS)]) for s in range(0, LS, TF)]
        tiles.append(ffn_tile(l, 1, [(LS, LP, False, False), (LS + LP, LP, False, False)]))
        drive(tiles)
    fw.barrier()

    nblk = NT // 128

    def fin_stage1(blk):
        return norm_A(XA, BXA, blk * 128, 128)

    def fin_stage2(blk, ctx):
        t0 = blk * 128
        st3, Bst, p_, Bp_, n_ = ctx
        norm_B(ctx, 0, 0, 2, (lambda c: st3[:, c, 0:128]), Bst)
        so, Bso = trst.get()
        for b4 in range(4):
            p, Bp = psAll.get()
            for q in range(4):
                c = b4 * 4 + q
                fw.op("pe", "transpose", out=p[:, q * 128:(q + 1) * 128], in_=st3[:, c, 0:128], identity=ident[:, :], reads=[Bst, Bident], writes=[Bp],
                      inc=(q == 3))
            if b4 % 2 == 0:
                fw.op("act", "copy", out=so[:, b4 * 512:(b4 + 1) * 512], in_=p[:, :], reads=[Bp], writes=[Bso])
            else:
                fw.op("dve", "tensor_copy", out=so[:, b4 * 512:(b4 + 1) * 512], in_=p[:, :], reads=[Bp], writes=[Bso])
        dst = ys_d[t0:t0 + 128, :] if t0 < LS else yp_d[t0 - LS:t0 - LS + 128, :]
        fw.dma("sp", dst, so[:, :], reads=[Bso], writes=[Buf()])

    ctx = fin_stage1(0)
    for blk in range(nblk):
        nxt = fin_stage1(blk + 1) if blk + 1 < nblk else None
        fin_stage2(blk, ctx)
        ctx = nxt

    fw.finish()
    fw.emit()
    return nc, fw


def _qperm():
    order = []
    for grp in range(2):
        for m in range(4):
            order += [grp * 8 + m, grp * 8 + 4 + m]
    cols = []
    for h in order:
        cols += list(range(h * 64, (h + 1) * 64))
    return np.array(cols + list(range(1024, 3584)), dtype=np.int64)


def _fm(v):
    v = np.asarray(v, dtype=np.float32)
    n = v.shape[-1] // 128
    v = v.reshape(v.shape[:-1] + (n, 128))
    return np.moveaxis(v, -1, 0)


def _consts(LS):
    ident = np.eye(128, dtype=np.float32)
    perm = np.zeros((128, 128), np.float32)
    for m in range(128):
        partner = m + 32 if (m % 64) < 32 else m - 32
        perm[partner, m] = 1.0
    rows = LS // 64
    row = np.repeat(np.arange(rows, dtype=np.float32), 64)
    colv = np.tile(np.arange(64, dtype=np.float32), rows)
    inv = (np.float32(10000.0) ** (-np.arange(16, dtype=np.float32) / np.float32(16))).astype(np.float32)
    ang = np.concatenate([row[:, None] * inv, colv[:, None] * inv], axis=-1).astype(np.float32)
    cos = np.cos(ang).astype(np.float32)
    sin = np.sin(ang).astype(np.float32)
    C = np.zeros((128, LS), np.float32)
    S = np.zeros((128, LS), np.float32)
    for p in range(128):
        d = p % 64
        C[p] = cos[:, d % 32]
        S[p] = -sin[:, d] if d < 32 else sin[:, d - 32]
    j = np.arange(128)[:, None]
    i = np.arange(128)[None, :]
    mP = np.tile((j >= i).astype(np.float32), (1, 4))
    mN = np.tile((j <= i).astype(np.float32), (1, 4))
    return dict(ident=ident, perm=perm, ropeC=C, ropeS=S, maskP=mP, maskN=mN)


def _pack(inp):
    pvec = np.zeros((128, NPV), np.float32)
    for l in range(2):
        b = l * PV_L
        pvec[:, b + 0:b + 16] = _fm(inp["g_norm1"][l])
        pvec[:, b + 16:b + 32] = _fm(inp["g_norm2"][l])
        pvec[:, b + 32:b + 128] = _fm(inp["b_ada"][l])
        pvec[:, b + 128:b + 392] = _fm(inp["ffn_dw_w"][l]).reshape(128, 3 * 88)
        pvec[:, b + 392:b + 480] = _fm(inp["ffn_dw_b"][l])
        pvec[:, b + 480:b + 604] = _fm(inp["conv_dw_w"][l]).reshape(128, 31 * 4)
        pvec[:, b + 604:b + 608] = _fm(inp["conv_dw_b"][l])
        pvec[:, b + 608:b + 612] = _fm(inp["conv_ln_g"][l])
        pvec[:, b + 612:b + 616] = _fm(inp["conv_ln_b"][l])
        pvec[:, b + 616:b + 620] = np.asarray(inp["gmlp_ln_g"][l], np.float32).T
        pvec[:, b + 620:b + 624] = np.asarray(inp["gmlp_ln_b"][l], np.float32).T
    pvec[:, 2 * PV_L:2 * PV_L + 16] = _fm(inp["g_final"])
    prow = np.zeros((2, NPR), np.float32)
    for l in range(2):
        prow[l, 0:512] = np.asarray(inp["gmlp_bs"][l], np.float32).reshape(-1)
        prow[l, 512:528] = np.asarray(inp["attn_sink"][l], np.float32).reshape(-1)
    return pvec, prow


_CACHE = {}


def kernel(**inp):
    inp = {k: np.asarray(v) for k, v in inp.items()}
    LS = inp["x_sample"].shape[1]
    n = 8
    if LS not in _CACHE:
        _CACHE[LS] = build(LS)[0]
    nc = _CACHE[LS]
    consts = _consts(LS)
    pvec, prow = _pack(inp)
    w_in = np.ascontiguousarray(inp["w_in"][:, :, _qperm()])
    shared = dict(pvec=pvec, prow=prow, w_ada=inp["w_ada"], w_in=w_in, w_out=inp["w_out"], w_up=inp["w_up"], w_down=inp["w_down"],
                  gmlp_ws=inp["gmlp_ws"], **consts)
    in_maps = []
    for c in range(n):
        cc = np.stack([inp["c"][c], inp["c_ctx"]], 0).astype(np.float32)
        ccT = np.ascontiguousarray(cc.reshape(2, 16, 128).transpose(2, 1, 0).reshape(128, 32))
        m = dict(shared)
        m.update(xs=np.ascontiguousarray(inp["x_sample"][c]),
                 xp=np.ascontiguousarray(inp["x_prompt"][2 * c:2 * c + 2].reshape(2 * LP, D)),
                 ck=np.ascontiguousarray(inp["cache_k"][c].reshape(2, 512, 256)),
                 cv=np.ascontiguousarray(inp["cache_v"][c].reshape(2, 512, 256)),
                 ccT=ccT)
        in_maps.append(m)
    res = run_bass_kernel_spmd(nc, in_maps, core_ids=list(range(n))).results
    y_sample = np.stack([r["ys"] for r in res], 0)
    y_prompt = np.concatenate([r["yp"].reshape(2, LP, D) for r in res], 0)
    nk = np.concatenate([r["nk"] for r in res], 0).reshape(16, 2, LP, 4, 64)
    nv = np.concatenate([r["nv"] for r in res], 0).reshape(16, 2, LP, 4, 64)
    return (y_prompt.astype(np.float32), y_sample.astype(np.float32), nk.astype(np.float32), nv.astype(np.float32))
```
